# Optimizing a Trainium2 kernel written in Bass

```python
import math
import jax, jax.numpy as jnp
from jax import lax
import numpy as np

D_MODEL = 2048
BATCH = 4
SEQ = 4096
DEPTH = 1

CHUNK = 64
N_META = 16
Q_BLOCK = 128
EPS = 1e-6
NEG_BIG = -1e30

MLA_HEADS = 8
Q_LORA = 512
KV_LORA = 512
NOPE_DIM = 128
ROPE_DIM = 64
V_DIM = 128
ROPE_THETA = 10000.0

SB_HEADS = 8
SB_HEAD_DIM = 128

A_WIDTH = MLA_HEADS * V_DIM
B_WIDTH = SB_HEADS * SB_HEAD_DIM

IN_SPLITS = (Q_LORA, KV_LORA, ROPE_DIM, B_WIDTH, B_WIDTH, B_WIDTH, D_MODEL, D_MODEL)
IN_COLS = sum(IN_SPLITS)

PEER_HEADS = 8
PEER_NKEYS = 128
PEER_N_EXPERTS = PEER_NKEYS * PEER_NKEYS
PEER_TOPK = 16
PEER_QDIM = 256
PEER_HALF = PEER_QDIM // 2
PEER_BLOCK = 128

kernel_name = "hybrid_mla_stickbreaking_peer_block"


def _rmsnorm(x, g):
    xf = x.astype(jnp.float32)
    out = xf * lax.rsqrt(jnp.mean(xf * xf, axis=-1, keepdims=True) + EPS)
    return (out * g.astype(jnp.float32)).astype(x.dtype)


def _rope(x, cos, sin):
    xf = x.astype(jnp.float32)
    x1, x2 = xf[..., : ROPE_DIM // 2], xf[..., ROPE_DIM // 2:]
    return jnp.concatenate([x1 * cos - x2 * sin, x1 * sin + x2 * cos], axis=-1).astype(x.dtype)


def _to_blocks(a, nb):
    return jnp.moveaxis(a.reshape(a.shape[0], nb, Q_BLOCK, *a.shape[2:]), 1, 0)


def _from_blocks(a):
    a = jnp.moveaxis(a, 0, 1)
    return a.reshape(a.shape[0], a.shape[1] * a.shape[2], *a.shape[3:])


def _mla_attention(q_nope, q_rope, k_nope, k_rope, v, chunk_id):
    nb = q_nope.shape[1] // Q_BLOCK
    scale = 1.0 / math.sqrt(NOPE_DIM + ROPE_DIM)

    def block(args):
        qn, qr, cq = args
        s = (jnp.einsum('bqhd,bkhd->bhqk', qn, k_nope)
             + jnp.einsum('bqhr,bkr->bhqk', qr, k_rope)).astype(jnp.float32) * scale
        visible = cq[:, None] >= chunk_id[None, :]
        s = jnp.where(visible, s, NEG_BIG)
        p = jax.nn.softmax(s, axis=-1).astype(v.dtype)
        return jnp.einsum('bhqk,bkhd->bqhd', p, v)

    out = lax.map(block, (_to_blocks(q_nope, nb), _to_blocks(q_rope, nb),
                          chunk_id.reshape(nb, Q_BLOCK)))
    return _from_blocks(out)


def _stick_breaking_attention(q, k, v, pos):
    nb = q.shape[1] // Q_BLOCK
    scale = 1.0 / math.sqrt(SB_HEAD_DIM)

    def block(args):
        qb, pq = args
        z = jnp.einsum('bqhd,bkhd->bhqk', qb, k).astype(jnp.float32) * scale
        strict = pq[:, None] > pos[None, :]
        log_beta = jax.nn.log_sigmoid(z)
        log_1m = jnp.where(strict, jax.nn.log_sigmoid(-z), 0.0)
        cs = jnp.cumsum(log_1m, axis=-1)
        rest = cs[..., -1:] - cs
        a = jnp.where(strict, jnp.exp(log_beta + rest), 0.0).astype(v.dtype)
        return jnp.einsum('bhqk,bkhd->bqhd', a, v)

    out = lax.map(block, (_to_blocks(q, nb), pos.reshape(nb, Q_BLOCK)))
    return _from_blocks(out)


def _peer(xn, w_q, sub_keys, u, v):
    b, lp, d = xn.shape
    n_tok = b * lp
    xt = xn.reshape(n_tok // PEER_BLOCK, PEER_BLOCK, d)

    def block(xb):
        q = (xb @ w_q).reshape(PEER_BLOCK, PEER_HEADS, 2, PEER_HALF)
        s = jnp.einsum('thcd,hcnd->thcn', q, sub_keys).astype(jnp.float32)
        s1, i1 = lax.top_k(s[:, :, 0], PEER_TOPK)
        s2, i2 = lax.top_k(s[:, :, 1], PEER_TOPK)
        cand_s = (s1[..., :, None] + s2[..., None, :]).reshape(PEER_BLOCK, PEER_HEADS, -1)
        cand_i = (i1[..., :, None] * PEER_NKEYS + i2[..., None, :]).reshape(PEER_BLOCK, PEER_HEADS, -1)
        top_s, top_pos = lax.top_k(cand_s, PEER_TOPK)
        idx = jnp.take_along_axis(cand_i, top_pos, axis=-1)
        g = jax.nn.softmax(top_s, axis=-1).reshape(PEER_BLOCK, -1)
        idx = idx.reshape(PEER_BLOCK, -1)
        h = jnp.einsum('tkd,td->tk', u[idx], xb)
        act = (g * jax.nn.gelu(h.astype(jnp.float32), approximate=False)).astype(xb.dtype)
        return jnp.einsum('tk,tkd->td', act, v[idx])

    return lax.map(block, xt).reshape(b, lp, d)


def setup_inputs(seed: int = 0) -> dict:
    key = jax.random.key(seed)
    ks = jax.random.split(key, 20)
    f32 = jnp.float32

    def nrm(k, shape, scale):
        return jax.random.normal(k, shape, f32) * scale

    def gain(k, shape):
        return 1.0 + 0.05 * jax.random.normal(k, shape, f32)

    offsets = jax.random.randint(ks[1], (BATCH, 1), 0, 64) * CHUNK
    positions = (offsets + jnp.arange(SEQ, dtype=jnp.int32)[None, :]).astype(jnp.int32)
    return {
        "x": nrm(ks[0], (BATCH, SEQ, D_MODEL), 1.0),
        "positions": positions,
        "meta_tokens": nrm(ks[2], (N_META, D_MODEL), 1.0),
        "norm_mix_g": gain(ks[3], (DEPTH, D_MODEL)),
        "w_in": nrm(ks[4], (DEPTH, D_MODEL, IN_COLS), D_MODEL ** -0.5),
        "mla_q_norm_g": gain(ks[5], (DEPTH, Q_LORA)),
        "mla_w_uq": nrm(ks[6], (DEPTH, Q_LORA, MLA_HEADS * (NOPE_DIM + ROPE_DIM)), Q_LORA ** -0.5),
        "mla_kv_norm_g": gain(ks[7], (DEPTH, KV_LORA)),
        "mla_w_uk": nrm(ks[8], (DEPTH, KV_LORA, MLA_HEADS * NOPE_DIM), KV_LORA ** -0.5),
        "mla_w_uv": nrm(ks[9], (DEPTH, KV_LORA, MLA_HEADS * V_DIM), KV_LORA ** -0.5),
        "w_branch_a": nrm(ks[10], (DEPTH, A_WIDTH, D_MODEL), A_WIDTH ** -0.5),
        "w_branch_b": nrm(ks[11], (DEPTH, B_WIDTH, D_MODEL), B_WIDTH ** -0.5),
        "w_out": nrm(ks[12], (DEPTH, D_MODEL, D_MODEL), D_MODEL ** -0.5),
        "norm_ffn_g": gain(ks[13], (DEPTH, D_MODEL)),
        "peer_w_q": nrm(ks[14], (DEPTH, D_MODEL, PEER_HEADS * PEER_QDIM), D_MODEL ** -0.5),
        "peer_sub_keys": nrm(ks[15], (DEPTH, PEER_HEADS, 2, PEER_NKEYS, PEER_HALF), PEER_HALF ** -0.5),
        "peer_u": nrm(ks[16], (DEPTH, PEER_N_EXPERTS, D_MODEL), D_MODEL ** -0.5),
        "peer_v": nrm(ks[17], (DEPTH, PEER_N_EXPERTS, D_MODEL), 0.2),
        "final_norm_g": gain(ks[18], (D_MODEL,)),
    }


def reference(x, positions, meta_tokens, norm_mix_g, w_in, mla_q_norm_g, mla_w_uq,
              mla_kv_norm_g, mla_w_uk, mla_w_uv, w_branch_a, w_branch_b, w_out,
              norm_ffn_g, peer_w_q, peer_sub_keys, peer_u, peer_v, final_norm_g):
    b, seq, d = x.shape
    length = seq + N_META
    lp = ((length + Q_BLOCK - 1) // Q_BLOCK) * Q_BLOCK
    n_pad = lp - length

    meta = jnp.broadcast_to(meta_tokens.astype(x.dtype)[None], (b, N_META, d))
    h = jnp.concatenate([meta, x, jnp.zeros((b, n_pad, d), x.dtype)], axis=1)

    idx = jnp.arange(lp, dtype=jnp.int32)
    chunk_id = jnp.where(idx < N_META, 0, 1 + (idx - N_META) // CHUNK)

    rot_pos = jnp.concatenate([
        jnp.broadcast_to(jnp.arange(N_META, dtype=jnp.int32)[None], (b, N_META)),
        N_META + positions,
        jnp.zeros((b, n_pad), jnp.int32)], axis=1)
    inv_freq = ROPE_THETA ** (-jnp.arange(ROPE_DIM // 2, dtype=jnp.float32) / (ROPE_DIM // 2))
    ang = rot_pos.astype(jnp.float32)[..., None] * inv_freq
    cos, sin = jnp.cos(ang), jnp.sin(ang)

    split_at = list(np.cumsum(IN_SPLITS)[:-1])
    for l in range(DEPTH):
        hn = _rmsnorm(h, norm_mix_g[l])
        proj = hn @ w_in[l]
        c_q, c_kv, k_rope, q_sb, k_sb, v_sb, gate_a, gate_b = jnp.split(proj, split_at, axis=-1)

        q = (_rmsnorm(c_q, mla_q_norm_g[l]) @ mla_w_uq[l]).reshape(b, lp, MLA_HEADS, NOPE_DIM + ROPE_DIM)
        q_nope = q[..., :NOPE_DIM]
        q_rope = _rope(q[..., NOPE_DIM:], cos[:, :, None], sin[:, :, None])
        ckv = _rmsnorm(c_kv, mla_kv_norm_g[l])
        k_nope = (ckv @ mla_w_uk[l]).reshape(b, lp, MLA_HEADS, NOPE_DIM)
        v_a = (ckv @ mla_w_uv[l]).reshape(b, lp, MLA_HEADS, V_DIM)
        k_rope = _rope(k_rope, cos, sin)
        y_a = _mla_attention(q_nope, q_rope, k_nope, k_rope, v_a, chunk_id).reshape(b, lp, A_WIDTH)

        y_b = _stick_breaking_attention(
            q_sb.reshape(b, lp, SB_HEADS, SB_HEAD_DIM),
            k_sb.reshape(b, lp, SB_HEADS, SB_HEAD_DIM),
            v_sb.reshape(b, lp, SB_HEADS, SB_HEAD_DIM), idx).reshape(b, lp, B_WIDTH)

        merged = (jax.nn.sigmoid(gate_a) * (y_a @ w_branch_a[l])
                  + jax.nn.sigmoid(gate_b) * (y_b @ w_branch_b[l]))
        h = h + merged @ w_out[l]

        h = h + _peer(_rmsnorm(h, norm_ffn_g[l]), peer_w_q[l], peer_sub_keys[l], peer_u[l], peer_v[l])

    h = _rmsnorm(h, final_norm_g)
    return h[:, N_META:N_META + seq]
```

```python
import contextlib
import math

import numpy as np
import concourse.bass as bass
import concourse.mybir as mybir
from concourse.bass_utils import run_bass_kernel_spmd

F32 = mybir.dt.float32
BF16 = mybir.dt.bfloat16
I32 = mybir.dt.int32
U32 = mybir.dt.uint32
AF = mybir.ActivationFunctionType
ALU = mybir.AluOpType
AX = mybir.AxisListType

D = 2048
N_META = 16
CHUNK = 64
EPS = 1e-6
NH = 8
QL = 512
KVL = 512
NOPE = 128
ROPE = 64
NEXP = 16384
TOPK = 16
NEG = -1.0e30


class Buf:
    __slots__ = ("t", "lw", "rd", "trk", "wl")

    def __init__(self, t, trk=True):
        self.t = t
        self.lw = None
        self.rd = []
        self.trk = trk
        self.wl = {}


class Eng:
    def __init__(self, k, name, eng, sem):
        self.k = k
        self.name = name
        self.eng = eng
        self.sem = sem
        self.cnt = 0
        self.waited = {}
        self.dsems = []
        self.dcnt = []
        self.di = 0


class K:
    def __init__(self, nc, es):
        self.nc = nc
        self.es = es
        self.sems = {}
        self.engs = {}
        for name, eng in (("pe", nc.tensor), ("act", nc.scalar), ("dve", nc.vector),
                          ("pool", nc.gpsimd), ("sp", nc.sync)):
            sem = es.enter_context(nc.semaphore("s_" + name))
            self.sems[id(sem)] = sem
            self.engs[name] = Eng(self, name, eng, sem)
        for name, n in (("sp", 24), ("pool", 24), ("act", 16)):
            e = self.engs[name]
            for i in range(n):
                sem = es.enter_context(nc.semaphore("d_%s%d" % (name, i)))
                self.sems[id(sem)] = sem
                e.dsems.append(sem)
                e.dcnt.append(0)
        self.nbuf = 0

    def sb(self, shape, dt, name=None):
        self.nbuf += 1
        return Buf(self.es.enter_context(self.nc.sbuf_tensor(name or ("sb%d" % self.nbuf), list(shape), dt)))

    def ps(self, shape, dt, name=None):
        self.nbuf += 1
        return Buf(self.es.enter_context(self.nc.psum_tensor(name or ("ps%d" % self.nbuf), list(shape), dt)))

    def dram(self, name, shape, dt, kind="Internal"):
        return Buf(self.nc.dram_tensor(name, list(shape), dt, kind=kind).ap(), trk=False)

    def _wait(self, E, ev):
        sid, val = ev
        if E.waited.get(sid, 0) >= val:
            return
        E.eng.wait_ge(self.sems[sid], val)
        E.waited[sid] = val

    def _deps(self, E, rd, wr):
        own = id(E.sem) if E.name == "pe" else None
        deps = {}

        def add(ev):
            if ev is None:
                return
            if deps.get(ev[0], 0) < ev[1]:
                deps[ev[0]] = ev[1]
        for b in rd:
            add(b.lw)
        for b in wr:
            if b.lw is not None and b.lw[0] != own:
                add(b.lw)
            for ev in b.rd:
                if ev[0] != own:
                    add(ev)
        for sid, val in deps.items():
            self._wait(E, (sid, val))

    def op(self, en, fn, rd=(), wr=(), sig=True):
        E = self.engs[en]
        rd = [b for b in rd if b.trk]
        wr = [b for b in wr if b.trk]
        self._deps(E, rd, wr)
        inst = fn(E.eng)
        if sig:
            E.cnt += 1
            inst.then_inc(E.sem, 1)
            ev = (id(E.sem), E.cnt)
        else:
            ev = (id(E.sem), E.cnt + 1)
        for b in rd:
            b.rd.append(ev)
            if len(b.rd) > 64:
                b.rd = b.rd[-48:]
        for b in wr:
            b.lw = ev
            b.rd = []
        return inst

    def dma(self, en, fn, rd=(), wr=()):
        E = self.engs[en]
        drd = [b for b in rd if not b.trk]
        dwr = [b for b in wr if not b.trk]
        rd = [b for b in rd if b.trk]
        wr = [b for b in wr if b.trk]
        i = E.di
        E.di = (E.di + 1) % len(E.dsems)
        sem = E.dsems[i]
        if E.dcnt[i] > 0:
            self._wait(E, (id(sem), E.dcnt[i]))
        for b in drd:
            for sid, val in b.wl.items():
                self._wait(E, (sid, val))
        self._deps(E, rd, wr)
        inst = fn(E.eng)
        E.dcnt[i] += 16
        inst.then_inc(sem, 16)
        ev = (id(sem), E.dcnt[i])
        for b in dwr:
            b.wl[ev[0]] = ev[1]
        for b in rd:
            b.rd.append(ev)
        for b in wr:
            b.lw = ev
            b.rd = []
        return inst

    def barrier(self):
        evs = []
        for E in self.engs.values():
            if E.cnt > 0:
                evs.append((id(E.sem), E.cnt))
            for sem, c in zip(E.dsems, E.dcnt):
                if c > 0:
                    evs.append((id(sem), c))
        for E in self.engs.values():
            for ev in evs:
                if ev[0] != id(E.sem):
                    self._wait(E, ev)


class Ring:
    def __init__(self, bufs):
        self.bufs = bufs
        self.i = 0

    def next(self):
        b = self.bufs[self.i]
        self.i = (self.i + 1) % len(self.bufs)
        return b


def build(T, debug=()):
    assert T % 1024 == 0
    OWN = T // 2
    NB = T // 128
    NTA = T + N_META
    NQT = OWN // 512
    nc = bass.Bass("TRN2", target_bir_lowering=False)
    es = contextlib.ExitStack()
    k = K(nc, es)
    dbg = set(debug)

    def din(name, shape, dt=F32):
        return Buf(nc.dram_tensor(name, list(shape), dt, kind="ExternalInput").ap(), trk=False)

    def scr(name, shape, dt):
        return k.dram(name, shape, dt, kind="ExternalOutput" if name in dbg else "Internal")

    x_all = din("x_all", [NTA, D])
    x_own = din("x_own", [OWN, D])
    pos_all = din("pos_all", [64, NTA], I32)
    pos_own = din("pos_own", [128, OWN], I32)
    cst = din("cst", [128, 8])
    g1b = din("g1b", [128, D])
    g2b = din("g2b", [128, D])
    gfb = din("gfb", [128, D])
    gqc = din("gqc", [128, 4])
    gkvc = din("gkvc", [128, 4])
    W = {}
    for nm, shp in (("wcq", [D, QL]), ("wckv", [D, KVL]), ("wkr", [D, ROPE]), ("wkrs", [D, ROPE]),
                    ("wqsb", [D, 1024]), ("wksb", [D, 1024]), ("wvsb", [D, 1024]),
                    ("wga", [D, D]), ("wgb", [D, D]),
                    ("wuqn", [QL, 1024]), ("wuqr", [QL, 512]), ("wuqrs", [QL, 512]),
                    ("wuk", [KVL, 1024]), ("wuv", [KVL, 1024]),
                    ("wa", [1024, D]), ("wb", [1024, D]), ("wout", [D, D]), ("pwq", [D, D])):
        W[nm] = din(nm, shp)
    keysT = din("keysT", [128, 16 * 128])
    peer_u = din("peer_u", [NEXP, D])
    peer_v = din("peer_v", [NEXP, D])
    m_sb = din("m_sb", [128, 8 * 512])
    m_mla = din("m_mla", [128, 8 * 512])
    c_ident = din("c_ident", [128, 128])
    c_tri = din("c_tri", [128, 128])
    c_iota = din("c_iota", [128, 16])
    y_out = Buf(nc.dram_tensor("y", [OWN, D], F32, kind="ExternalOutput").ap(), trk=False)

    HNT_ALL = scr("HNT_ALL", [D, NTA], BF16)
    HNT_OWN = scr("HNT_OWN", [D, OWN], BF16)
    CKV_T = scr("CKV_T", [KVL, NTA], F32)
    KR1 = scr("KR1", [ROPE, NTA], F32)
    KRR_T = scr("KRR_T", [ROPE, NTA], BF16)
    KSB_T = scr("KSB_T", [1024, NTA], BF16)
    VSB = scr("VSB", [NTA, 1024], BF16)
    CQ_T = scr("CQ_T", [QL, OWN], F32)
    QSB_T = scr("QSB_T", [1024, OWN], BF16)
    GA_T = scr("GA_T", [D, OWN], F32)
    GB_T = scr("GB_T", [D, OWN], F32)
    CQN_T = scr("CQN_T", [QL, OWN], BF16)
    CKVN_T = scr("CKVN_T", [KVL, NTA], BF16)
    QN_T = scr("QN_T", [1024, OWN], BF16)
    QR1 = scr("QR1", [512, OWN], F32)
    QR_T = scr("QR_T", [512, OWN], BF16)
    KN_T = scr("KN_T", [1024, NTA], BF16)
    VA = scr("VA", [NTA, 1024], BF16)
    TCK = scr("TCK", [64, NTA], F32)
    TSK = scr("TSK", [64, NTA], F32)
    TCQ = scr("TCQ", [128, OWN], F32)
    TSQ = scr("TSQ", [128, OWN], F32)
    YA_T = scr("YA_T", [1024, OWN], BF16)
    YB_T = scr("YB_T", [1024, OWN], BF16)
    M1_T = scr("M1_T", [D, OWN], F32)
    MG_T = scr("MG_T", [D, OWN], BF16)
    H_OWN = scr("H_OWN", [OWN, D], F32)
    XNT = scr("XNT", [D, OWN], BF16)
    PQ_T = scr("PQ_T", [D, OWN], BF16)
    UV16 = scr("UV16", [NEXP, 2 * D], BF16)

    all_tiles = [(i * 512, 512) for i in range(T // 512)] + [(T, N_META)]
    own_tiles = [(i * 512, 512) for i in range(OWN // 512)]

    with es:
        ident = k.sb([128, 128], BF16, "ident")
        tri_f = k.sb([128, 128], F32, "tri_f")
        ones_f = k.sb([128, 128], F32, "ones_f")
        ones_b = k.sb([128, 128], BF16, "ones_b")
        iota16 = k.sb([128, 16], F32, "iota16")
        cst_s = k.sb([128, 8], F32, "cst_s")
        gq_s = k.sb([128, 4], F32, "gq_s")
        gkv_s = k.sb([128, 4], F32, "gkv_s")
        eps_s = k.sb([128, 1], F32, "eps_s")
        k.dma("pool", lambda e: e.dma_start(out=ident.t[:], in_=c_ident.t[:, :]), wr=[ident])
        k.dma("sp", lambda e: e.dma_start(out=tri_f.t[:], in_=c_tri.t[:, :]), wr=[tri_f])
        k.dma("sp", lambda e: e.dma_start(out=iota16.t[:], in_=c_iota.t[:, :]), wr=[iota16])
        k.dma("sp", lambda e: e.dma_start(out=cst_s.t[:], in_=cst.t[:, :]), wr=[cst_s])
        k.dma("sp", lambda e: e.dma_start(out=gq_s.t[:], in_=gqc.t[:, :]), wr=[gq_s])
        k.dma("sp", lambda e: e.dma_start(out=gkv_s.t[:], in_=gkvc.t[:, :]), wr=[gkv_s])
        k.op("dve", lambda e: e.memset(ones_f.t[:], 1.0), wr=[ones_f])
        k.op("dve", lambda e: e.memset(ones_b.t[:], 1.0), wr=[ones_b])
        k.op("dve", lambda e: e.memset(eps_s.t[:], EPS), wr=[eps_s])

        banks = [k.ps([128, 512], F32, "bank%d" % i) for i in range(8)]

        def rstd_from_ss(ss, np_, n, sq, rstd):
            k.op("act", lambda e: e.activation(out=sq.t[:np_], in_=ss.t[:np_], func=AF.Sqrt,
                                               scale=1.0 / n, bias=eps_s.t[:np_, 0:1]),
                 rd=[ss, eps_s], wr=[sq])
            k.op("dve", lambda e: e.reciprocal(out=rstd.t[:np_], in_=sq.t[:np_]), rd=[sq], wr=[rstd])

        def phase_norm_T(src, gain_dram, groups, outT):
            with contextlib.ExitStack() as ls:
                k.es, old = ls, k.es
                gb = k.sb([128, D], F32)
                xr = Ring([k.sb([128, D], F32) for _ in range(2)])
                hr = Ring([k.sb([128, D], BF16) for _ in range(2)])
                junk = k.sb([128, D], BF16)
                gr = Ring([k.sb([128, 16, 512], BF16) for _ in range(2)])
                st = Ring([(k.sb([128, 1], F32), k.sb([128, 1], F32), k.sb([128, 1], F32)) for _ in range(2)])
                pr = Ring([(banks[0], banks[1]), (banks[2], banks[3])])
                k.dma("sp", lambda e: e.dma_start(out=gb.t[:], in_=gain_dram.t[:, :]), wr=[gb])
                outv = outT.t.rearrange("(c p) t -> p c t", p=128)
                for (t0, gt) in groups:
                    grp = gr.next()
                    ntile = (gt + 127) // 128
                    for q in range(ntile):
                        r0 = t0 + q * 128
                        np_ = min(128, gt - q * 128)
                        xt = xr.next()
                        hn = hr.next()
                        ss, sq, rstd = st.next()
                        pa, pb = pr.next()
                        k.dma("sp", lambda e: e.dma_start(out=xt.t[:np_], in_=src.t[r0:r0 + np_, :]), rd=[src], wr=[xt])
                        k.op("act", lambda e: e.activation(out=junk.t[:np_], in_=xt.t[:np_], func=AF.Square,
                                                           accum_out=ss.t[:np_, 0:1]), rd=[xt], wr=[junk, ss])
                        rstd_from_ss(ss, np_, D, sq, rstd)
                        k.op("dve", lambda e: e.scalar_tensor_tensor(out=hn.t[:np_], in0=xt.t[:np_],
                                                                     scalar=rstd.t[:np_, 0:1], in1=gb.t[:np_],
                                                                     op0=ALU.mult, op1=ALU.mult),
                             rd=[xt, rstd, gb], wr=[hn])
                        for half, pbank in ((0, pa), (1, pb)):
                            pv = pbank.t[:].bitcast(BF16)
                            for c8 in range(8):
                                c = half * 8 + c8
                                k.op("pe", lambda e: e.transpose(out=pv[:, c8 * 128:c8 * 128 + np_],
                                                                 in_=hn.t[:np_, c * 128:(c + 1) * 128],
                                                                 identity=ident.t[:np_, :np_]),
                                     rd=[hn, ident], wr=[pbank], sig=(c8 == 7))
                            src_v = pv.rearrange("p (c t) -> p c t", t=128)[:, :, :np_]
                            dst_v = grp.t[:, half * 8:(half + 1) * 8, q * 128:q * 128 + np_]
                            if half == 0:
                                k.op("act", lambda e: e.copy(out=dst_v, in_=src_v), rd=[pbank], wr=[grp])
                            else:
                                k.op("dve", lambda e: e.tensor_copy(out=dst_v, in_=src_v), rd=[pbank], wr=[grp])
                    k.dma("act", lambda e: e.dma_start(out=outv[:, :, t0:t0 + gt], in_=grp.t[:, :, :gt]),
                          rd=[grp], wr=[outT])
                k.barrier()
                k.es = old

        def gemm(Wd, Kdim, n_lo, n_hi, actT, tiles, out, mode, epi="copy", odt=BF16,
                 ea=None, eb=None, ea_mod=None, out_row0=0):
            KC = Kdim // 128
            if True:
                wr_, ar_, orr32, orr16, tmp, ear, ebr, pr = GW
                orr = orr16 if odt == BF16 else orr32
                Wv = Wd.t.rearrange("(c p) n -> p c n", p=128)
                Av = actT.t.rearrange("(c p) t -> p c t", p=128)
                flip = [0]
                for s0 in range(n_lo, n_hi, 512):
                    ns = min(512, n_hi - s0)
                    wt = wr_.next()
                    k.dma("pool", lambda e: e.dma_start(out=wt.t[:, :KC, :ns], in_=Wv[:, :, s0:s0 + ns]), wr=[wt])
                    for (t0, tt) in tiles:
                        at = ar_.next()
                        k.dma("sp", lambda e: e.dma_start(out=at.t[:, :KC, :tt], in_=Av[:, :, t0:t0 + tt]), rd=[actT], wr=[at])
                        if mode == "fm":
                            subs = [(nb, min(128, ns - nb)) for nb in range(0, ns, 128)]
                        else:
                            subs = [(tb, min(128, tt - tb)) for tb in range(0, tt, 128)]
                        for (o0, on) in subs:
                            ps = pr.next()
                            if mode == "fm":
                                rows, cols = on, tt
                                for c in range(KC):
                                    k.op("pe", lambda e: e.matmul(ps.t[:rows, :cols], lhsT=wt.t[:, c, o0:o0 + on],
                                                                  rhs=at.t[:, c, :tt], start=(c == 0), stop=(c == KC - 1)),
                                         rd=[wt, at], wr=[ps], sig=(c == KC - 1))
                                orow = out_row0 + (s0 - n_lo) + o0
                                dst = out.t[orow:orow + rows, t0:t0 + cols]
                                erow, ecol = orow, t0
                            else:
                                rows, cols = on, ns
                                for c in range(KC):
                                    k.op("pe", lambda e: e.matmul(ps.t[:rows, :cols], lhsT=at.t[:, c, o0:o0 + on],
                                                                  rhs=wt.t[:, c, :ns], start=(c == 0), stop=(c == KC - 1)),
                                         rd=[wt, at], wr=[ps], sig=(c == KC - 1))
                                ocol = out_row0 + (s0 - n_lo)
                                dst = out.t[t0 + o0:t0 + o0 + rows, ocol:ocol + cols]
                                erow, ecol = t0 + o0, ocol
                            ob = orr.next()
                            if epi in ("mul", "muladd", "add"):
                                a_t = ear.next()
                                ar0 = erow % ea_mod if ea_mod else erow
                                k.dma("sp", lambda e: e.dma_start(out=a_t.t[:rows, :cols],
                                                                  in_=ea.t[ar0:ar0 + rows, ecol:ecol + cols]),
                                      rd=[ea], wr=[a_t])
                            if epi == "muladd":
                                b_t = ebr.next()
                                k.dma("sp", lambda e: e.dma_start(out=b_t.t[:rows, :cols],
                                                                  in_=eb.t[erow:erow + rows, ecol:ecol + cols]),
                                      rd=[eb], wr=[b_t])
                            if epi == "copy":
                                flip[0] ^= 1
                                if flip[0]:
                                    k.op("act", lambda e: e.copy(out=ob.t[:rows, :cols], in_=ps.t[:rows, :cols]),
                                         rd=[ps], wr=[ob])
                                else:
                                    k.op("dve", lambda e: e.tensor_copy(out=ob.t[:rows, :cols], in_=ps.t[:rows, :cols]),
                                         rd=[ps], wr=[ob])
                            elif epi == "sigmoid":
                                k.op("act", lambda e: e.activation(out=ob.t[:rows, :cols], in_=ps.t[:rows, :cols],
                                                                   func=AF.Sigmoid), rd=[ps], wr=[ob])
                            elif epi == "mul":
                                k.op("dve", lambda e: e.tensor_tensor(out=ob.t[:rows, :cols], in0=ps.t[:rows, :cols],
                                                                      in1=a_t.t[:rows, :cols], op=ALU.mult),
                                     rd=[ps, a_t], wr=[ob])
                            elif epi == "add":
                                k.op("dve", lambda e: e.tensor_tensor(out=ob.t[:rows, :cols], in0=ps.t[:rows, :cols],
                                                                      in1=a_t.t[:rows, :cols], op=ALU.add),
                                     rd=[ps, a_t], wr=[ob])
                            elif epi == "muladd":
                                tb_ = tmp.next()
                                k.op("dve", lambda e: e.tensor_tensor(out=tb_.t[:rows, :cols], in0=ps.t[:rows, :cols],
                                                                      in1=a_t.t[:rows, :cols], op=ALU.mult),
                                     rd=[ps, a_t], wr=[tb_])
                                k.op("pool", lambda e: e.tensor_tensor(out=ob.t[:rows, :cols], in0=tb_.t[:rows, :cols],
                                                                       in1=b_t.t[:rows, :cols], op=ALU.add),
                                     rd=[tb_, b_t], wr=[ob])
                            k.dma("act", lambda e: e.dma_start(out=dst, in_=ob.t[:rows, :cols]), rd=[ob], wr=[out])

        def phase_fm_norm(srcT, gcol, tiles, outT):
            if True:
                xr, sqr, rr, orr = FW
                pr = GW[7]
                sv = srcT.t.rearrange("(c p) t -> p c t", p=128)
                ov = outT.t.rearrange("(c p) t -> p c t", p=128)
                for (t0, tt) in tiles:
                    xt = xr.next(); sq = sqr.next(); (sr, rs) = rr.next(); ob = orr.next(); ps = pr.next()
                    k.dma("sp", lambda e: e.dma_start(out=xt.t[:, :, :tt], in_=sv[:, :, t0:t0 + tt]), rd=[srcT], wr=[xt])
                    k.op("act", lambda e: e.activation(out=sq.t[:, :, :tt], in_=xt.t[:, :, :tt], func=AF.Square),
                         rd=[xt], wr=[sq])
                    for c in range(4):
                        k.op("pe", lambda e: e.matmul(ps.t[:, :tt], lhsT=ones_f.t[:, :], rhs=sq.t[:, c, :tt],
                                                      start=(c == 0), stop=(c == 3)),
                             rd=[ones_f, sq], wr=[ps], sig=(c == 3))
                    k.op("act", lambda e: e.activation(out=sr.t[:, :tt], in_=ps.t[:, :tt], func=AF.Sqrt,
                                                       scale=1.0 / 512, bias=eps_s.t[:, 0:1]), rd=[ps, eps_s], wr=[sr])
                    k.op("dve", lambda e: e.reciprocal(out=rs.t[:, :tt], in_=sr.t[:, :tt]), rd=[sr], wr=[rs])
                    for c in range(4):
                        k.op("dve", lambda e: e.scalar_tensor_tensor(out=ob.t[:, c, :tt], in0=xt.t[:, c, :tt],
                                                                     scalar=gcol.t[:, c:c + 1], in1=rs.t[:, :tt],
                                                                     op0=ALU.mult, op1=ALU.mult),
                             rd=[xt, gcol, rs], wr=[ob])
                    k.dma("act", lambda e: e.dma_start(out=ov[:, :, t0:t0 + tt], in_=ob.t[:, :, :tt]), rd=[ob], wr=[outT])

        def phase_tables(pos, nP, NT, tc_out, ts_out):
            C1 = 6.28125
            C2 = 2.0 * math.pi - C1
            with contextlib.ExitStack() as ls:
                k.es, old = ls, k.es
                W_ = 512
                pi_ = Ring([k.sb([128, W_], I32) for _ in range(2)])
                ni_ = Ring([k.sb([128, W_], I32) for _ in range(2)])
                f = [Ring([k.sb([128, W_], F32) for _ in range(2)]) for _ in range(8)]
                for t0 in range(0, NT, W_):
                    tt = min(W_, NT - t0)
                    pi = pi_.next(); ni = ni_.next()
                    pf, ang, nf, x1, x2, xs, xc, o = [r.next() for r in f]
                    sl = lambda b: b.t[:nP, :tt]
                    k.dma("sp", lambda e: e.dma_start(out=sl(pi), in_=pos.t[:nP, t0:t0 + tt]), wr=[pi])
                    k.op("dve", lambda e: e.tensor_copy(out=sl(pf), in_=sl(pi)), rd=[pi], wr=[pf])
                    k.op("dve", lambda e: e.tensor_scalar(out=sl(ang), in0=sl(pf), scalar1=float(N_META),
                                                          scalar2=cst_s.t[:nP, 0:1], op0=ALU.add, op1=ALU.mult),
                         rd=[pf, cst_s], wr=[ang])
                    k.op("dve", lambda e: e.tensor_scalar(out=sl(ni), in0=sl(ang), scalar1=1.0 / (2.0 * math.pi),
                                                          scalar2=None, op0=ALU.mult), rd=[ang], wr=[ni])
                    k.op("dve", lambda e: e.tensor_copy(out=sl(nf), in_=sl(ni)), rd=[ni], wr=[nf])
                    k.op("dve", lambda e: e.scalar_tensor_tensor(out=sl(x1), in0=sl(nf), scalar=-C1, in1=sl(ang),
                                                                 op0=ALU.mult, op1=ALU.add), rd=[nf, ang], wr=[x1])
                    k.op("dve", lambda e: e.scalar_tensor_tensor(out=sl(x2), in0=sl(nf), scalar=-C2, in1=sl(x1),
                                                                 op0=ALU.mult, op1=ALU.add), rd=[nf, x1], wr=[x2])
                    k.op("dve", lambda e: e.tensor_scalar(out=sl(o), in0=sl(x2), scalar1=math.pi, scalar2=-2.0 * math.pi,
                                                          op0=ALU.is_gt, op1=ALU.mult), rd=[x2], wr=[o])
                    k.op("dve", lambda e: e.tensor_tensor(out=sl(xs), in0=sl(x2), in1=sl(o), op=ALU.add), rd=[x2, o], wr=[xs])
                    k.op("dve", lambda e: e.tensor_scalar(out=sl(pf), in0=sl(x2), scalar1=math.pi / 2, scalar2=None,
                                                          op0=ALU.add), rd=[x2], wr=[pf])
                    k.op("dve", lambda e: e.tensor_scalar(out=sl(o), in0=sl(pf), scalar1=math.pi, scalar2=-2.0 * math.pi,
                                                          op0=ALU.is_gt, op1=ALU.mult), rd=[pf], wr=[o])
                    k.op("dve", lambda e: e.tensor_tensor(out=sl(xc), in0=sl(pf), in1=sl(o), op=ALU.add), rd=[pf, o], wr=[xc])
                    k.op("dve", lambda e: e.tensor_scalar(out=sl(xs), in0=sl(xs), scalar1=math.pi, scalar2=-math.pi,
                                                          op0=ALU.min, op1=ALU.max), rd=[xs], wr=[xs])
                    k.op("dve", lambda e: e.tensor_scalar(out=sl(xc), in0=sl(xc), scalar1=math.pi, scalar2=-math.pi,
                                                          op0=ALU.min, op1=ALU.max), rd=[xc], wr=[xc])
                    k.op("act", lambda e: e.activation(out=sl(o), in_=sl(xc), func=AF.Sin), rd=[xc], wr=[o])
                    k.dma("sp", lambda e: e.dma_start(out=tc_out.t[:nP, t0:t0 + tt], in_=sl(o)), rd=[o], wr=[tc_out])
                    k.op("act", lambda e: e.activation(out=sl(x1), in_=sl(xs), func=AF.Sin), rd=[xs], wr=[x1])
                    k.op("dve", lambda e: e.tensor_scalar(out=sl(x2), in0=sl(x1), scalar1=cst_s.t[:nP, 1:2],
                                                          scalar2=None, op0=ALU.mult), rd=[x1, cst_s], wr=[x2])
                    k.dma("sp", lambda e: e.dma_start(out=ts_out.t[:nP, t0:t0 + tt], in_=sl(x2)), rd=[x2], wr=[ts_out])
                k.barrier()
                k.es = old

        def alloc_gemm_ws():
            return (Ring([k.sb([128, 16, 512], BF16) for _ in range(2)]),
                    Ring([k.sb([128, 16, 512], BF16) for _ in range(2)]),
                    Ring([k.sb([128, 512], F32) for _ in range(3)]),
                    Ring([k.sb([128, 512], BF16) for _ in range(3)]),
                    Ring([k.sb([128, 512], F32) for _ in range(2)]),
                    Ring([k.sb([128, 512], F32) for _ in range(3)]),
                    Ring([k.sb([128, 512], F32) for _ in range(3)]),
                    Ring(banks))

        def alloc_fm_ws():
            return (Ring([k.sb([128, 4, 512], F32) for _ in range(2)]),
                    Ring([k.sb([128, 4, 512], F32) for _ in range(2)]),
                    Ring([(k.sb([128, 512], F32), k.sb([128, 512], F32)) for _ in range(2)]),
                    Ring([k.sb([128, 4, 512], BF16) for _ in range(2)]))

        phase_norm_T(x_all, g1b, all_tiles, HNT_ALL)
        phase_norm_T(x_own, g1b, own_tiles, HNT_OWN)
        phase_tables(pos_all, 64, NTA, TCK, TSK)
        phase_tables(pos_own, 128, OWN, TCQ, TSQ)

        sec = contextlib.ExitStack()
        k.es, old_es = sec, k.es
        GW = alloc_gemm_ws()
        FW = alloc_fm_ws()
        gemm(W["wckv"], D, 0, KVL, HNT_ALL, all_tiles, CKV_T, "fm", odt=F32)
        gemm(W["wkr"], D, 0, ROPE, HNT_ALL, all_tiles, KR1, "fm", epi="mul", odt=F32, ea=TCK)
        gemm(W["wkrs"], D, 0, ROPE, HNT_ALL, all_tiles, KRR_T, "fm", epi="muladd", odt=BF16, ea=TSK, eb=KR1)
        gemm(W["wksb"], D, 0, 1024, HNT_ALL, all_tiles, KSB_T, "fm", odt=BF16)
        gemm(W["wvsb"], D, 0, 1024, HNT_ALL, all_tiles, VSB, "tm", odt=BF16)
        gemm(W["wcq"], D, 0, QL, HNT_OWN, own_tiles, CQ_T, "fm", odt=F32)
        gemm(W["wqsb"], D, 0, 1024, HNT_OWN, own_tiles, QSB_T, "fm", odt=BF16)
        gemm(W["wga"], D, 0, D, HNT_OWN, own_tiles, GA_T, "fm", epi="sigmoid", odt=F32)
        gemm(W["wgb"], D, 0, D, HNT_OWN, own_tiles, GB_T, "fm", epi="sigmoid", odt=F32)

        phase_fm_norm(CQ_T, gq_s, own_tiles, CQN_T)
        phase_fm_norm(CKV_T, gkv_s, all_tiles, CKVN_T)
        gemm(W["wuqn"], QL, 0, 1024, CQN_T, own_tiles, QN_T, "fm", odt=BF16)
        gemm(W["wuqr"], QL, 0, 512, CQN_T, own_tiles, QR1, "fm", epi="mul", odt=F32, ea=TCQ, ea_mod=128)
        gemm(W["wuqrs"], QL, 0, 512, CQN_T, own_tiles, QR_T, "fm", epi="muladd", odt=BF16, ea=TSQ, ea_mod=128, eb=QR1)
        gemm(W["wuk"], KVL, 0, 1024, CKVN_T, all_tiles, KN_T, "fm", odt=BF16)
        gemm(W["wuv"], KVL, 0, 1024, CKVN_T, all_tiles, VA, "tm", odt=BF16)
        k.barrier()
        sec.close()
        k.es = old_es

        sc_mla = 1.0 / math.sqrt(NOPE + ROPE)
        sc_sb = 1.0 / math.sqrt(128.0)
        with contextlib.ExitStack() as ls:
            k.es, old = ls, k.es
            NBA = NB + 1
            msb_f = k.sb([128, 8, 512], BF16)
            mml_f = k.sb([128, 8, 512], BF16)
            k.dma("pool", lambda e: e.dma_start(out=msb_f.t[:], in_=m_sb.t.rearrange("p (j q) -> p j q", q=512)), wr=[msb_f])
            k.dma("pool", lambda e: e.dma_start(out=mml_f.t[:], in_=m_mla.t.rearrange("p (j q) -> p j q", q=512)), wr=[mml_f])
            KR = k.sb([64, NTA], BF16)
            k.dma("sp", lambda e: e.dma_start(out=KR.t[:], in_=KRR_T.t[:, :]), rd=[KRR_T], wr=[KR])
            hbs = [dict(QS=k.sb([128, OWN], BF16), KS=k.sb([128, NTA], BF16), VS=k.sb([128, NBA, 128], BF16),
                        QN=k.sb([128, OWN], BF16), QR=k.sb([64, OWN], BF16), KN=k.sb([128, NTA], BF16),
                        VAh=k.sb([128, NBA, 128], BF16), YA=k.sb([128, OWN], BF16), YB=k.sb([128, OWN], BF16))
                   for _ in range(2)]
            Pr = Ring([k.sb([128, 512], BF16) for _ in range(3)])
            Er = Ring([k.sb([128, 512], F32) for _ in range(3)])
            Lr = Ring([k.sb([128, 512], F32) for _ in range(3)])
            Xr = Ring([k.sb([128, 512], F32) for _ in range(2)])
            Ar = Ring([k.sb([128, 512], BF16) for _ in range(3)])
            Lacc = Ring([k.sb([128, 512], F32) for _ in range(3)])
            rz = k.sb([128, 512], F32)
            Sm = Ring([banks[0], banks[1]])
            Ss = Ring([banks[2], banks[3]])
            pI = banks[4]
            Ob, Zb, Yb = banks[5], banks[6], banks[7]
            VSBv = VSB.t[0:T, :].rearrange("(j p) n -> p j n", p=128)
            VAv = VA.t[0:T, :].rearrange("(j p) n -> p j n", p=128)
            stg_r = Ring([k.sb([128, 2 * D], BF16) for _ in range(3)])

            def conv_chunk(c):
                e0 = c * 128
                stg = stg_r.next()
                k.dma("pool", lambda e: e.dma_start(out=stg.t[:, 0:D], in_=peer_u.t[e0:e0 + 128, :]), wr=[stg])
                k.dma("pool", lambda e: e.dma_start(out=stg.t[:, D:2 * D], in_=peer_v.t[e0:e0 + 128, :]), wr=[stg])
                k.dma("sp", lambda e: e.dma_start(out=UV16.t[e0:e0 + 128, :], in_=stg.t[:]), rd=[stg], wr=[UV16])

            def load_head(h):
                hb = hbs[h % 2]
                hs = slice(h * 128, (h + 1) * 128)
                k.dma("sp", lambda e: e.dma_start(out=hb["QS"].t[:], in_=QSB_T.t[hs, :]), rd=[QSB_T], wr=[hb["QS"]])
                k.dma("sp", lambda e: e.dma_start(out=hb["KS"].t[:], in_=KSB_T.t[hs, :]), rd=[KSB_T], wr=[hb["KS"]])
                k.dma("sp", lambda e: e.dma_start(out=hb["VS"].t[:, 0:NB, :], in_=VSBv[:, :, hs]), rd=[VSB], wr=[hb["VS"]])
                k.dma("sp", lambda e: e.dma_start(out=hb["VS"].t[:N_META, NB, :], in_=VSB.t[T:NTA, hs]), rd=[VSB], wr=[hb["VS"]])
                k.dma("sp", lambda e: e.dma_start(out=hb["QN"].t[:], in_=QN_T.t[hs, :]), rd=[QN_T], wr=[hb["QN"]])
                k.dma("sp", lambda e: e.dma_start(out=hb["QR"].t[:], in_=QR_T.t[h * 64:(h + 1) * 64, :]), rd=[QR_T], wr=[hb["QR"]])
                k.dma("sp", lambda e: e.dma_start(out=hb["KN"].t[:], in_=KN_T.t[hs, :]), rd=[KN_T], wr=[hb["KN"]])
                k.dma("sp", lambda e: e.dma_start(out=hb["VAh"].t[:, 0:NB, :], in_=VAv[:, :, hs]), rd=[VA], wr=[hb["VAh"]])
                k.dma("sp", lambda e: e.dma_start(out=hb["VAh"].t[:N_META, NB, :], in_=VA.t[T:NTA, hs]), rd=[VA], wr=[hb["VAh"]])

            steps = []
            for h in range(NH):
                for I in range(NQT):
                    sl_ = [(8 * I + jj, jj) for jj in range(7, -1, -1)] + [(j, None) for j in range(8 * I - 1, -1, -1)] + [(NB, None)]
                    for si, (j, jj) in enumerate(sl_):
                        steps.append(dict(h=h, I=I, j=j, jj=jj, first=(si == 0), last=(si == len(sl_) - 1),
                                          kp=(N_META if j == NB else 128), k0=j * 128,
                                          qs=slice(I * 512, (I + 1) * 512), hb=hbs[h % 2]))

            def stA(st):
                hb, kp, k0, qs = st["hb"], st["kp"], st["k0"], st["qs"]
                pS = Sm.next(); pS2 = Ss.next()
                st["pS"], st["pS2"] = pS, pS2
                k.op("pe", lambda e: e.matmul(pS.t[:kp, :], lhsT=hb["KN"].t[:, k0:k0 + kp], rhs=hb["QN"].t[:, qs],
                                              start=True, stop=False), rd=[hb["KN"], hb["QN"]], wr=[pS], sig=False)
                k.op("pe", lambda e: e.matmul(pS.t[:kp, :], lhsT=KR.t[:, k0:k0 + kp], rhs=hb["QR"].t[:, qs],
                                              start=False, stop=True), rd=[KR, hb["QR"]], wr=[pS])
                k.op("pe", lambda e: e.matmul(pS2.t[:kp, :], lhsT=hb["KS"].t[:, k0:k0 + kp], rhs=hb["QS"].t[:, qs],
                                              start=True, stop=True), rd=[hb["KS"], hb["QS"]], wr=[pS2])

            def stB_act(st):
                kp, jj = st["kp"], st["jj"]
                pS, pS2 = st["pS"], st["pS2"]
                Pb = Pr.next(); Ef = Er.next(); Lf = Lr.next()
                st["Pb"], st["Ef"], st["Lf"] = Pb, Ef, Lf
                k.op("act", lambda e: e.activation(out=Pb.t[:kp], in_=pS.t[:kp, :], func=AF.Exp, scale=sc_mla),
                     rd=[pS], wr=[Pb])
                k.op("act", lambda e: e.activation(out=Ef.t[:kp], in_=pS2.t[:kp, :], func=AF.Exp, scale=sc_sb),
                     rd=[pS2], wr=[Ef])
                k.op("act", lambda e: e.activation(out=Lf.t[:kp], in_=Ef.t[:kp], func=AF.Ln, bias=ones_f.t[:kp, 0:1]),
                     rd=[Ef, ones_f], wr=[Lf])
                if jj is not None:
                    k.op("pool", lambda e: e.tensor_tensor(out=Pb.t[:kp], in0=Pb.t[:kp], in1=mml_f.t[:kp, jj, :],
                                                           op=ALU.mult), rd=[Pb, mml_f], wr=[Pb])
                    k.op("pool", lambda e: e.tensor_tensor(out=Lf.t[:kp], in0=Lf.t[:kp], in1=msb_f.t[:kp, jj, :],
                                                           op=ALU.mult), rd=[Lf, msb_f], wr=[Lf])

            def stB_pe(st, la_prev):
                hb, kp, j = st["hb"], st["kp"], st["j"]
                first, last = st["first"], st["last"]
                Pb, Lf = st["Pb"], st["Lf"]
                k.op("pe", lambda e: e.matmul(Ob.t[:, :], lhsT=hb["VAh"].t[:kp, j, :], rhs=Pb.t[:kp],
                                              start=first, stop=last), rd=[hb["VAh"], Pb], wr=[Ob], sig=False)
                k.op("pe", lambda e: e.matmul(Zb.t[:, :], lhsT=ones_b.t[:kp, :], rhs=Pb.t[:kp],
                                              start=first, stop=last), rd=[ones_b, Pb], wr=[Zb])
                k.op("pe", lambda e: e.matmul(pI.t[:kp, :], lhsT=tri_f.t[:kp, :kp], rhs=Lf.t[:kp],
                                              start=True, stop=first), rd=[tri_f, Lf], wr=[pI], sig=first)
                if not first:
                    k.op("pe", lambda e: e.matmul(pI.t[:kp, :], lhsT=ones_f.t[:, :kp], rhs=la_prev.t[:, :],
                                                  start=False, stop=True), rd=[ones_f, la_prev], wr=[pI])
                la = None
                if not last:
                    la = Lacc.next()
                    if first:
                        k.op("dve", lambda e: e.tensor_copy(out=la.t[:], in_=Lf.t[:]), rd=[Lf], wr=[la])
                    else:
                        k.op("dve", lambda e: e.tensor_tensor(out=la.t[:], in0=la_prev.t[:], in1=Lf.t[:], op=ALU.add),
                             rd=[la_prev, Lf], wr=[la])
                return la

            def stC(st):
                kp, jj = st["kp"], st["jj"]
                Ef = st["Ef"]
                Xf = Xr.next(); Ab = Ar.next()
                st["Ab"] = Ab
                k.op("act", lambda e: e.activation(out=Xf.t[:kp], in_=pI.t[:kp, :], func=AF.Exp, scale=-1.0),
                     rd=[pI], wr=[Xf])
                k.op("dve", lambda e: e.tensor_tensor(out=Ab.t[:kp], in0=Ef.t[:kp], in1=Xf.t[:kp], op=ALU.mult),
                     rd=[Ef, Xf], wr=[Ab])
                if jj is not None:
                    k.op("pool", lambda e: e.tensor_tensor(out=Ab.t[:kp], in0=Ab.t[:kp], in1=msb_f.t[:kp, jj, :],
                                                           op=ALU.mult), rd=[Ab, msb_f], wr=[Ab])

            def fin_mla(st):
                hb, qs = st["hb"], st["qs"]
                k.op("dve", lambda e: e.reciprocal(out=rz.t[:], in_=Zb.t[:, :]), rd=[Zb], wr=[rz])
                k.op("dve", lambda e: e.tensor_tensor(out=hb["YA"].t[:, qs], in0=Ob.t[:, :], in1=rz.t[:], op=ALU.mult),
                     rd=[Ob, rz], wr=[hb["YA"]])

            def stD(st):
                hb, kp, j, qs = st["hb"], st["kp"], st["j"], st["qs"]
                Ab = st["Ab"]
                k.op("pe", lambda e: e.matmul(Yb.t[:, :], lhsT=hb["VS"].t[:kp, j, :], rhs=Ab.t[:kp],
                                              start=st["first"], stop=st["last"]), rd=[hb["VS"], Ab], wr=[Yb])
                if st["last"]:
                    k.op("act", lambda e: e.copy(out=hb["YB"].t[:, qs], in_=Yb.t[:, :]), rd=[Yb], wr=[hb["YB"]])
                    if st["I"] == NQT - 1:
                        h = st["h"]
                        hs = slice(h * 128, (h + 1) * 128)
                        k.dma("act", lambda e: e.dma_start(out=YA_T.t[hs, :], in_=hb["YA"].t[:]), rd=[hb["YA"]], wr=[YA_T])
                        k.dma("act", lambda e: e.dma_start(out=YB_T.t[hs, :], in_=hb["YB"].t[:]), rd=[hb["YB"]], wr=[YB_T])

            load_head(0)
            n = len(steps)
            stA(steps[0])
            stB_act(steps[0])
            la_prev = None
            NCH = NEXP // 128
            for i in range(n):
                st = steps[i]
                for c in range((NCH * i) // n, (NCH * (i + 1)) // n):
                    conv_chunk(c)
                if i + 1 < n:
                    stA(steps[i + 1])
                    stB_act(steps[i + 1])
                la_prev = stB_pe(st, la_prev)
                if st["last"]:
                    fin_mla(st)
                stC(st)
                if i >= 1:
                    stD(steps[i - 1])
                    sp_ = steps[i - 1]
                    if sp_["first"] and sp_["I"] == 0 and sp_["h"] + 1 < NH:
                        load_head(sp_["h"] + 1)
            stD(steps[n - 1])
            k.barrier()
            k.es = old

        sec = contextlib.ExitStack()
        k.es, old_es = sec, k.es
        GW = alloc_gemm_ws()
        gemm(W["wa"], 1024, 0, D, YA_T, own_tiles, M1_T, "fm", epi="mul", odt=F32, ea=GA_T)
        gemm(W["wb"], 1024, 0, D, YB_T, own_tiles, MG_T, "fm", epi="muladd", odt=BF16, ea=GB_T, eb=M1_T)
        gemm(W["wout"], D, 0, D, MG_T, own_tiles, H_OWN, "tm", epi="add", odt=F32, ea=x_own)
        k.barrier()
        sec.close()
        k.es = old_es
        phase_norm_T(H_OWN, g2b, own_tiles, XNT)
        sec = contextlib.ExitStack()
        k.es, old_es = sec, k.es
        GW = alloc_gemm_ws()
        gemm(W["pwq"], D, 0, D, XNT, own_tiles, PQ_T, "fm", odt=BF16)
        k.barrier()
        sec.close()
        k.es = old_es

        with contextlib.ExitStack() as ls:
            k.es, old = ls, k.es
            kT = k.sb([128, 16, 128], BF16)
            k.dma("pool", lambda e: e.dma_start(out=kT.t[:], in_=keysT.t.rearrange("p (g n) -> p g n", n=128)), wr=[kT])
            g2s = k.sb([128, D], F32)
            gfs = k.sb([128, D], F32)
            k.dma("sp", lambda e: e.dma_start(out=g2s.t[:], in_=g2b.t[:, :]), wr=[g2s])
            k.dma("sp", lambda e: e.dma_start(out=gfs.t[:], in_=gfb.t[:, :]), wr=[gfs])
            qTr = Ring([k.sb([128, 16, 128], BF16) for _ in range(2)])
            hr_ = Ring([k.sb([128, D], F32) for _ in range(2)])
            xnr = Ring([k.sb([128, D], F32) for _ in range(2)])
            junkr = Ring([k.sb([128, D], BF16) for _ in range(3)])
            junka = k.sb([128, D], BF16)

            def subs(b, n):
                return [Buf(b.t) for _ in range(n)]
            s_sb = k.sb([128, 16, 128], F32)
            s2_sb = k.sb([128, 16, 128], F32)
            vtop = k.sb([128, 16, 16], F32)
            itop = k.sb([128, 16, 16], U32)
            itopf = k.sb([128, 16, 16], F32)
            cand = k.sb([128, 8, 256], F32)
            cand2 = k.sb([128, 8, 256], F32)
            vfin = k.sb([128, 8, 16], F32)
            pfin = k.sb([128, 8, 16], U32)
            pi_u = k.sb([128, 8, 16], U32)
            pj_u = k.sb([128, 8, 16], U32)
            pi_f = k.sb([128, 8, 16], F32)
            pj_f = k.sb([128, 8, 16], F32)
            I1 = k.sb([128, 128], F32)
            I2 = k.sb([128, 128], F32)
            idxf = k.sb([128, 128], F32)
            idxr = Ring([k.sb([128, 128], I32) for _ in range(2)])
            ge = k.sb([128, 8, 16], F32)
            gz = k.sb([128, 8], F32)
            grz = k.sb([128, 8], F32)
            gater = Ring([k.sb([128, 128], F32) for _ in range(2)])
            GS = 4
            NG = 128 // GS
            hd_t = k.sb([128, 128], F32)
            gl_t = k.sb([128, 128], F32)
            av_t = k.sb([128, 128], F32)
            hd_s = subs(hd_t, 128)
            s_sb4 = subs(s_sb, 4); s2g = subs(s2_sb, 16); vtg = subs(vtop, 16); itg = subs(itop, 16)
            cdh = subs(cand, 8); cd2h = subs(cand2, 8); vfh = subs(vfin, 8); pfh = subs(pfin, 8)
            gl_g = [Buf(gl_t.t) for _ in range(NG)]
            av_g = [Buf(av_t.t) for _ in range(NG)]
            dgr = Ring([k.sb([128, 128], BF16) for _ in range(6)])
            gbuf = Ring([k.sb([128, 2 * D], BF16) for _ in range(8)])
            st = Ring([(k.sb([128, 1], F32), k.sb([128, 1], F32), k.sb([128, 1], F32)) for _ in range(2)])
            yt = k.sb([128, D], F32)
            accb = banks[4:8]
            PQv = PQ_T.t.rearrange("(g p) t -> p g t", p=128)
            NTB = OWN // 128

            def peer_prep(tb):
                r0 = tb * 128
                qT = qTr.next(); ht = hr_.next(); xn = xnr.next()
                k.dma("sp", lambda e: e.dma_start(out=qT.t[:], in_=PQv[:, :, r0:r0 + 128]), rd=[PQ_T], wr=[qT])
                k.dma("sp", lambda e: e.dma_start(out=ht.t[:], in_=H_OWN.t[r0:r0 + 128, :]), rd=[H_OWN], wr=[ht])
                ss, sq, rstd = st.next()
                k.op("act", lambda e: e.activation(out=junka.t[:], in_=ht.t[:], func=AF.Square, accum_out=ss.t[:, 0:1]),
                     rd=[ht], wr=[junka, ss])
                rstd_from_ss(ss, 128, D, sq, rstd)
                k.op("dve", lambda e: e.scalar_tensor_tensor(out=xn.t[:], in0=ht.t[:], scalar=rstd.t[:, 0:1], in1=g2s.t[:],
                                                             op0=ALU.mult, op1=ALU.mult), rd=[ht, rstd, g2s], wr=[xn])
                for g4 in range(4):
                    pb = banks[g4]
                    for gg in range(4):
                        g = g4 * 4 + gg
                        k.op("pe", lambda e: e.matmul(pb.t[:, gg * 128:(gg + 1) * 128], lhsT=qT.t[:, g, :], rhs=kT.t[:, g, :],
                                                      start=True, stop=True), rd=[qT, kT], wr=[pb], sig=(gg == 3))
                    k.op("act", lambda e: e.copy(out=s_sb.t[:, g4 * 4:(g4 + 1) * 4, :],
                                                 in_=pb.t[:, :].rearrange("p (g n) -> p g n", n=128)), rd=[pb], wr=[s_sb4[g4]])
                for g in range(16):
                    k.op("dve", lambda e: e.max(out=vtop.t[:, g, 0:8], in_=s_sb.t[:, g, :]), rd=[s_sb4[g // 4]], wr=[vtg[g]])
                for g in range(16):
                    k.op("dve", lambda e: e.max_index(out=itop.t[:, g, 0:8], in_max=vtop.t[:, g, 0:8], in_values=s_sb.t[:, g, :]),
                         rd=[s_sb4[g // 4], vtg[g]], wr=[itg[g]])
                for g in range(16):
                    k.op("dve", lambda e: e.match_replace(out=s2_sb.t[:, g, :], in_to_replace=vtop.t[:, g, 0:8],
                                                          in_values=s_sb.t[:, g, :], imm_value=NEG), rd=[s_sb4[g // 4], vtg[g]], wr=[s2g[g]])
                for g in range(16):
                    k.op("dve", lambda e: e.max(out=vtop.t[:, g, 8:16], in_=s2_sb.t[:, g, :]), rd=[s2g[g]], wr=[vtg[g]])
                for g in range(16):
                    k.op("dve", lambda e: e.max_index(out=itop.t[:, g, 8:16], in_max=vtop.t[:, g, 8:16], in_values=s2_sb.t[:, g, :]),
                         rd=[s2g[g], vtg[g]], wr=[itg[g]])
                k.op("dve", lambda e: e.tensor_copy(out=itopf.t[:], in_=itop.t[:]), rd=itg, wr=[itopf])
                vt4 = vtop.t[:].rearrange("p (h c) k -> p h c k", c=2)
                it4 = itopf.t[:].rearrange("p (h c) k -> p h c k", c=2)
                for h in range(NH):
                    k.op("dve", lambda e: e.tensor_tensor(out=cand.t[:, h, :].rearrange("p (i j) -> p i j", j=16),
                                                          in0=vt4[:, h, 0, :].unsqueeze(2).to_broadcast([128, 16, 16]),
                                                          in1=vt4[:, h, 1, :].unsqueeze(1).to_broadcast([128, 16, 16]),
                                                          op=ALU.add), rd=[vtg[2 * h], vtg[2 * h + 1]], wr=[cdh[h]])
                for h in range(NH):
                    k.op("dve", lambda e: e.max(out=vfin.t[:, h, 0:8], in_=cand.t[:, h, :]), rd=[cdh[h]], wr=[vfh[h]])
                for h in range(NH):
                    k.op("dve", lambda e: e.max_index(out=pfin.t[:, h, 0:8], in_max=vfin.t[:, h, 0:8], in_values=cand.t[:, h, :]),
                         rd=[cdh[h], vfh[h]], wr=[pfh[h]])
                for h in range(NH):
                    k.op("dve", lambda e: e.match_replace(out=cand2.t[:, h, :], in_to_replace=vfin.t[:, h, 0:8],
                                                          in_values=cand.t[:, h, :], imm_value=NEG), rd=[cdh[h], vfh[h]], wr=[cd2h[h]])
                for h in range(NH):
                    k.op("dve", lambda e: e.max(out=vfin.t[:, h, 8:16], in_=cand2.t[:, h, :]), rd=[cd2h[h]], wr=[vfh[h]])
                for h in range(NH):
                    k.op("dve", lambda e: e.max_index(out=pfin.t[:, h, 8:16], in_max=vfin.t[:, h, 8:16], in_values=cand2.t[:, h, :]),
                         rd=[cd2h[h], vfh[h]], wr=[pfh[h]])
                k.op("dve", lambda e: e.tensor_scalar(out=pi_u.t[:], in0=pfin.t[:], scalar1=4, scalar2=None,
                                                      op0=ALU.logical_shift_right), rd=pfh, wr=[pi_u])
                k.op("dve", lambda e: e.tensor_scalar(out=pj_u.t[:], in0=pfin.t[:], scalar1=15, scalar2=None,
                                                      op0=ALU.bitwise_and), rd=pfh, wr=[pj_u])
                k.op("dve", lambda e: e.tensor_copy(out=pi_f.t[:], in_=pi_u.t[:]), rd=[pi_u], wr=[pi_f])
                k.op("dve", lambda e: e.tensor_copy(out=pj_f.t[:], in_=pj_u.t[:]), rd=[pj_u], wr=[pj_f])
                for (pf_, c_, Iout, ohb, ohs) in ((pi_f, 0, I1, cand, cdh), (pj_f, 1, I2, cand2, cd2h)):
                    ohv = ohb.t[:].rearrange("p h (k i) -> p h k i", i=16)
                    for h in range(NH):
                        k.op("dve", lambda e: e.tensor_tensor(out=ohv[:, h, :, :],
                                                              in0=pf_.t[:, h, :].unsqueeze(2).to_broadcast([128, 16, 16]),
                                                              in1=iota16.t[:, :].unsqueeze(1).to_broadcast([128, 16, 16]),
                                                              op=ALU.is_equal), rd=[pf_, iota16], wr=[ohs[h]])
                    for h in range(NH):
                        k.op("dve", lambda e: e.tensor_tensor(out=ohv[:, h, :, :], in0=ohv[:, h, :, :],
                                                              in1=it4[:, h, c_, :].unsqueeze(1).to_broadcast([128, 16, 16]),
                                                              op=ALU.mult), rd=[ohs[h], itopf], wr=[ohs[h]])
                    k.op("dve", lambda e: e.tensor_reduce(out=Iout.t[:, :], in_=ohb.t[:].rearrange("p h (k i) -> p (h k) i", i=16),
                                                          axis=AX.X, op=ALU.add), rd=ohs, wr=[Iout])
                k.op("dve", lambda e: e.scalar_tensor_tensor(out=idxf.t[:], in0=I1.t[:], scalar=128.0, in1=I2.t[:],
                                                             op0=ALU.mult, op1=ALU.add), rd=[I1, I2], wr=[idxf])
                idx = idxr.next()
                k.op("dve", lambda e: e.tensor_copy(out=idx.t[:], in_=idxf.t[:]), rd=[idxf], wr=[idx])
                k.op("dve", lambda e: e.tensor_tensor(out=ge.t[:], in0=vfin.t[:], in1=vfin.t[:, :, 0:1].to_broadcast([128, 8, 16]),
                                                      op=ALU.subtract), rd=vfh, wr=[ge])
                k.op("act", lambda e: e.activation(out=ge.t[:], in_=ge.t[:], func=AF.Exp), rd=[ge], wr=[ge])
                k.op("dve", lambda e: e.tensor_reduce(out=gz.t[:, :], in_=ge.t[:], axis=AX.X, op=ALU.add), rd=[ge], wr=[gz])
                k.op("dve", lambda e: e.reciprocal(out=grz.t[:], in_=gz.t[:]), rd=[gz], wr=[grz])
                gate = gater.next()
                k.op("dve", lambda e: e.tensor_tensor(out=gate.t[:].rearrange("p (h k) -> p h k", k=16), in0=ge.t[:],
                                                      in1=grz.t[:].unsqueeze(2).to_broadcast([128, 8, 16]), op=ALU.mult),
                     rd=[ge, grz], wr=[gate])
                return dict(ht=ht, xn=xn, idx=idx, gate=gate, r0=r0)

            def grp_fetch(pp, g):
                bufs = []
                for s_ in range(g * GS, (g + 1) * GS):
                    gb_ = gbuf.next()
                    k.dma("pool", lambda e: e.indirect_dma_start(out=gb_.t[:], out_offset=None, in_=UV16.t[:, :],
                                                                 in_offset=bass.IndirectOffsetOnAxis(ap=pp["idx"].t[:, s_:s_ + 1], axis=0)),
                          rd=[pp["idx"], UV16], wr=[gb_])
                    jk = junkr.next()
                    k.op("dve", lambda e: e.scalar_tensor_tensor(out=jk.t[:], in0=gb_.t[:, 0:D], scalar=1.0, in1=pp["xn"].t[:],
                                                                 op0=ALU.mult, op1=ALU.mult, accum_out=hd_t.t[:, s_:s_ + 1]),
                         rd=[gb_, pp["xn"]], wr=[jk, hd_s[s_]])
                    bufs.append(gb_)
                return bufs

            def grp_finish(pp, g, bufs):
                gs = slice(g * GS, (g + 1) * GS)
                k.op("act", lambda e: e.activation(out=gl_g[g].t[:, gs], in_=hd_t.t[:, gs], func=AF.Gelu),
                     rd=hd_s[g * GS:(g + 1) * GS], wr=[gl_g[g]])
                k.op("dve", lambda e: e.tensor_tensor(out=av_g[g].t[:, gs], in0=gl_g[g].t[:, gs], in1=pp["gate"].t[:, gs], op=ALU.mult),
                     rd=[gl_g[g], pp["gate"]], wr=[av_g[g]])
                for q, s_ in enumerate(range(g * GS, (g + 1) * GS)):
                    dg = dgr.next()
                    k.op("act", lambda e: e.activation(out=dg.t[:], in_=ident.t[:], func=AF.Copy, scale=av_g[g].t[:, s_:s_ + 1]),
                         rd=[ident, av_g[g]], wr=[dg])
                    for n4 in range(4):
                        k.op("pe", lambda e: e.matmul(accb[n4].t[:, :], lhsT=dg.t[:, :], rhs=bufs[q].t[:, D + n4 * 512:D + (n4 + 1) * 512],
                                                      start=(s_ == 0), stop=(s_ == 127)), rd=[dg, bufs[q]], wr=[accb[n4]], sig=(n4 == 3))

            def peer_out(pp):
                for n4 in range(4):
                    cs = slice(n4 * 512, (n4 + 1) * 512)
                    k.op("dve", lambda e: e.tensor_tensor(out=yt.t[:, cs], in0=accb[n4].t[:, :], in1=pp["ht"].t[:, cs], op=ALU.add),
                         rd=[accb[n4], pp["ht"]], wr=[yt])
                ss, sq, rstd = st.next()
                k.op("act", lambda e: e.activation(out=junka.t[:], in_=yt.t[:], func=AF.Square, accum_out=ss.t[:, 0:1]),
                     rd=[yt], wr=[junka, ss])
                rstd_from_ss(ss, 128, D, sq, rstd)
                k.op("dve", lambda e: e.scalar_tensor_tensor(out=yt.t[:], in0=yt.t[:], scalar=rstd.t[:, 0:1], in1=gfs.t[:],
                                                             op0=ALU.mult, op1=ALU.mult), rd=[yt, rstd, gfs], wr=[yt])
                r0 = pp["r0"]
                k.dma("sp", lambda e: e.dma_start(out=y_out.t[r0:r0 + 128, :], in_=yt.t[:]), rd=[yt], wr=[y_out])

            pp = peer_prep(0)
            for tb in range(NTB):
                pp_next = None
                prev = None
                for g in range(NG + 1):
                    cur = grp_fetch(pp, g) if g < NG else None
                    if prev is not None:
                        grp_finish(pp, g - 1, prev)
                    prev = cur
                    if g == NG // 2 and tb + 1 < NTB:
                        pp_next = peer_prep(tb + 1)
                peer_out(pp)
                pp = pp_next
            k.barrier()
            k.es = old
    return nc


def _consts(T):
    p = np.arange(128)
    inv_freq = (10000.0 ** (-(np.arange(32, dtype=np.float32)) / 32.0)).astype(np.float32)
    cst = np.zeros((128, 8), np.float32)
    cst[:, 0] = inv_freq[p % 32]
    cst[:, 1] = np.where((p % 64) < 32, -1.0, 1.0)
    ident = np.eye(128, dtype=np.float32)
    tri = (p[:, None] >= p[None, :]).astype(np.float32)
    iota = np.tile(np.arange(16, dtype=np.float32)[None], (128, 1))
    return cst, ident, tri, iota


def _masks(s):
    kl = np.arange(128)[:, None]
    ql = np.arange(128)[None, :]
    msb = np.zeros((128, 8, 512), np.float32)
    mml = np.zeros((128, 8, 512), np.float32)
    for jj in range(8):
        for r in range(4):
            g = 2 * r + s
            cs = slice(r * 128, (r + 1) * 128)
            if jj < g:
                msb[:, jj, cs] = 1.0
                mml[:, jj, cs] = 1.0
            elif jj == g:
                msb[:, jj, cs] = (kl < ql)
                mml[:, jj, cs] = ((kl // CHUNK) <= (ql // CHUNK))
    return msb.reshape(128, 8 * 512), mml.reshape(128, 8 * 512)


def _rep(v, n=128):
    return np.ascontiguousarray(np.broadcast_to(np.asarray(v, np.float32)[None, :], (n, v.shape[0])))


def _col(v):
    v = np.asarray(v, np.float32)
    return np.ascontiguousarray(v.reshape(-1, 128).T)


def prep_inputs(inp, B, T):
    f = lambda a: np.ascontiguousarray(np.asarray(a, np.float32))
    x = np.asarray(inp["x"], np.float32)
    pos = np.asarray(inp["positions"]).astype(np.int32)
    meta = f(inp["meta_tokens"])
    w_in = f(inp["w_in"][0])
    o = np.cumsum([0, 512, 512, 64, 1024, 1024, 1024, 2048, 2048])
    seg = [w_in[:, o[i]:o[i + 1]] for i in range(8)]
    sw64 = np.concatenate([np.arange(32, 64), np.arange(0, 32)])
    uq = f(inp["mla_w_uq"][0]).reshape(QL, NH, NOPE + ROPE)
    uqn = uq[:, :, :NOPE].reshape(QL, NH * NOPE)
    uqr = uq[:, :, NOPE:]
    cst, ident, tri, iota = _consts(T)
    shared = {
        "cst": cst, "c_ident": ident, "c_tri": tri, "c_iota": iota,
        "g1b": _rep(inp["norm_mix_g"][0]), "g2b": _rep(inp["norm_ffn_g"][0]), "gfb": _rep(inp["final_norm_g"]),
        "gqc": _col(inp["mla_q_norm_g"][0]), "gkvc": _col(inp["mla_kv_norm_g"][0]),
        "wcq": f(seg[0]), "wckv": f(seg[1]), "wkr": f(seg[2]), "wkrs": f(seg[2][:, sw64]),
        "wqsb": f(seg[3]), "wksb": f(seg[4]), "wvsb": f(seg[5]), "wga": f(seg[6]), "wgb": f(seg[7]),
        "wuqn": f(uqn), "wuqr": f(uqr.reshape(QL, NH * ROPE)), "wuqrs": f(uqr[:, :, sw64].reshape(QL, NH * ROPE)),
        "wuk": f(inp["mla_w_uk"][0]), "wuv": f(inp["mla_w_uv"][0]),
        "wa": f(inp["w_branch_a"][0]), "wb": f(inp["w_branch_b"][0]), "wout": f(inp["w_out"][0]),
        "pwq": f(inp["peer_w_q"][0]),
        "keysT": f(np.transpose(np.asarray(inp["peer_sub_keys"][0], np.float32).reshape(16, 128, 128), (2, 0, 1)).reshape(128, 16 * 128)),
        "peer_u": f(inp["peer_u"][0]), "peer_v": f(inp["peer_v"][0]),
    }
    maps = []
    metapos = (np.arange(N_META) - N_META).astype(np.int32)
    for c in range(2 * B):
        b, s = c // 2, c % 2
        xb = x[b].reshape(T // 128, 128, D)
        pb = pos[b].reshape(T // 128, 128)
        m = dict(shared)
        m["x_all"] = np.ascontiguousarray(np.concatenate([x[b], meta], axis=0))
        m["x_own"] = np.ascontiguousarray(xb[s::2].reshape(T // 2, D))
        pa = np.concatenate([pos[b], metapos])
        m["pos_all"] = np.ascontiguousarray(np.broadcast_to(pa[None], (64, T + N_META))).astype(np.int32)
        po = pb[s::2].reshape(T // 2)
        m["pos_own"] = np.ascontiguousarray(np.broadcast_to(po[None], (128, T // 2))).astype(np.int32)
        msb, mml = _masks(s)
        m["m_sb"] = msb
        m["m_mla"] = mml
        maps.append(m)
    return maps


def run(inp, debug=()):
    x = np.asarray(inp["x"])
    B, T, _ = x.shape
    nc = build(T, debug=debug)
    maps = prep_inputs(inp, B, T)
    res = run_bass_kernel_spmd(nc, maps, core_ids=list(range(2 * B)))
    out = np.zeros((B, T, D), np.float32)
    for c in range(2 * B):
        b, s = c // 2, c % 2
        out[b].reshape(T // 128, 128, D)[s::2] = res.results[c]["y"].reshape(T // 256, 128, D)
    return out, res


def kernel(**inputs):
    out, _ = run(inputs)
    return out
```

```python
import contextlib
import math

import numpy as np
import concourse.bass as bass
import concourse.mybir as mybir
from concourse.bass_utils import run_bass_kernel_spmd

F32 = mybir.dt.float32
BF16 = mybir.dt.bfloat16
I32 = mybir.dt.int32
U32 = mybir.dt.uint32
AF = mybir.ActivationFunctionType
ALU = mybir.AluOpType
AX = mybir.AxisListType

D = 2048
N_META = 16
CHUNK = 64
EPS = 1e-6
NH = 8
QL = 512
KVL = 512
NOPE = 128
ROPE = 64
NEXP = 16384
TOPK = 16
NEG = -1.0e30


class Buf:
    __slots__ = ("t", "lw", "rd", "trk", "wl")

    def __init__(self, t, trk=True):
        self.t = t
        self.lw = None
        self.rd = []
        self.trk = trk
        self.wl = {}


class Eng:
    def __init__(self, k, name, eng, sem):
        self.k = k
        self.name = name
        self.eng = eng
        self.sem = sem
        self.cnt = 0
        self.waited = {}
        self.dsems = []
        self.dcnt = []
        self.di = 0


class K:
    def __init__(self, nc, es):
        self.nc = nc
        self.es = es
        self.sems = {}
        self.engs = {}
        for name, eng in (("pe", nc.tensor), ("act", nc.scalar), ("dve", nc.vector),
                          ("pool", nc.gpsimd), ("sp", nc.sync)):
            sem = es.enter_context(nc.semaphore("s_" + name))
            self.sems[id(sem)] = sem
            self.engs[name] = Eng(self, name, eng, sem)
        for name, n in (("sp", 24), ("pool", 24), ("act", 16)):
            e = self.engs[name]
            for i in range(n):
                sem = es.enter_context(nc.semaphore("d_%s%d" % (name, i)))
                self.sems[id(sem)] = sem
                e.dsems.append(sem)
                e.dcnt.append(0)
        self.nbuf = 0

    def sb(self, shape, dt, name=None):
        self.nbuf += 1
        return Buf(self.es.enter_context(self.nc.sbuf_tensor(name or ("sb%d" % self.nbuf), list(shape), dt)))

    def ps(self, shape, dt, name=None):
        self.nbuf += 1
        return Buf(self.es.enter_context(self.nc.psum_tensor(name or ("ps%d" % self.nbuf), list(shape), dt)))

    def dram(self, name, shape, dt, kind="Internal"):
        return Buf(self.nc.dram_tensor(name, list(shape), dt, kind=kind).ap(), trk=False)

    def _wait(self, E, ev):
        sid, val = ev
        if E.waited.get(sid, 0) >= val:
            return
        E.eng.wait_ge(self.sems[sid], val)
        E.waited[sid] = val

    def _deps(self, E, rd, wr):
        own = id(E.sem) if E.name == "pe" else None
        deps = {}

        def add(ev):
            if ev is None:
                return
            if deps.get(ev[0], 0) < ev[1]:
                deps[ev[0]] = ev[1]
        for b in rd:
            add(b.lw)
        for b in wr:
            if b.lw is not None and b.lw[0] != own:
                add(b.lw)
            for ev in b.rd:
                if ev[0] != own:
                    add(ev)
        for sid, val in deps.items():
            self._wait(E, (sid, val))

    def op(self, en, fn, rd=(), wr=(), sig=True):
        E = self.engs[en]
        rd = [b for b in rd if b.trk]
        wr = [b for b in wr if b.trk]
        self._deps(E, rd, wr)
        inst = fn(E.eng)
        if sig:
            E.cnt += 1
            inst.then_inc(E.sem, 1)
            ev = (id(E.sem), E.cnt)
        else:
            ev = (id(E.sem), E.cnt + 1)
        for b in rd:
            b.rd.append(ev)
            if len(b.rd) > 64:
                b.rd = b.rd[-48:]
        for b in wr:
            b.lw = ev
            b.rd = []
        return inst

    def dma(self, en, fn, rd=(), wr=()):
        E = self.engs[en]
        drd = [b for b in rd if not b.trk]
        dwr = [b for b in wr if not b.trk]
        rd = [b for b in rd if b.trk]
        wr = [b for b in wr if b.trk]
        i = E.di
        E.di = (E.di + 1) % len(E.dsems)
        sem = E.dsems[i]
        if E.dcnt[i] > 0:
            self._wait(E, (id(sem), E.dcnt[i]))
        for b in drd:
            for sid, val in b.wl.items():
                self._wait(E, (sid, val))
        self._deps(E, rd, wr)
        inst = fn(E.eng)
        E.dcnt[i] += 16
        inst.then_inc(sem, 16)
        ev = (id(sem), E.dcnt[i])
        for b in dwr:
            b.wl[ev[0]] = ev[1]
        for b in rd:
            b.rd.append(ev)
        for b in wr:
            b.lw = ev
            b.rd = []
        return inst

    def barrier(self):
        evs = []
        for E in self.engs.values():
            if E.cnt > 0:
                evs.append((id(E.sem), E.cnt))
            for sem, c in zip(E.dsems, E.dcnt):
                if c > 0:
                    evs.append((id(sem), c))
        for E in self.engs.values():
            for ev in evs:
                if ev[0] != id(E.sem):
                    self._wait(E, ev)


class Ring:
    def __init__(self, bufs):
        self.bufs = bufs
        self.i = 0

    def next(self):
        b = self.bufs[self.i]
        self.i = (self.i + 1) % len(self.bufs)
        return b


def build(T, debug=()):
    assert T % 1024 == 0
    OWN = T // 2
    NB = T // 128
    NTA = T + N_META
    NQT = OWN // 512
    nc = bass.Bass("TRN2", target_bir_lowering=False)
    es = contextlib.ExitStack()
    k = K(nc, es)
    dbg = set(debug)

    def din(name, shape, dt=F32):
        return Buf(nc.dram_tensor(name, list(shape), dt, kind="ExternalInput").ap(), trk=False)

    def scr(name, shape, dt):
        return k.dram(name, shape, dt, kind="ExternalOutput" if name in dbg else "Internal")

    x_all = din("x_all", [NTA, D])
    x_own = din("x_own", [OWN, D])
    pos_all = din("pos_all", [64, NTA], I32)
    pos_own = din("pos_own", [128, OWN], I32)
    cst = din("cst", [128, 8])
    g1b = din("g1b", [128, D])
    g2b = din("g2b", [128, D])
    gfb = din("gfb", [128, D])
    gqc = din("gqc", [128, 4])
    gkvc = din("gkvc", [128, 4])
    W = {}
    for nm, shp in (("wcq", [D, QL]), ("wckv", [D, KVL]), ("wkr", [D, ROPE]), ("wkrs", [D, ROPE]),
                    ("wqsb", [D, 1024]), ("wksb", [D, 1024]), ("wvsb", [D, 1024]),
                    ("wga", [D, D]), ("wgb", [D, D]),
                    ("wuqn", [QL, 1024]), ("wuqr", [QL, 512]), ("wuqrs", [QL, 512]),
                    ("wuk", [KVL, 1024]), ("wuv", [KVL, 1024]),
                    ("wa", [1024, D]), ("wb", [1024, D]), ("wout", [D, D]), ("pwq", [D, D])):
        W[nm] = din(nm, shp)
    keysT = din("keysT", [128, 16 * 128])
    peer_u = din("peer_u", [NEXP, D])
    peer_v = din("peer_v", [NEXP, D])
    m_sb = din("m_sb", [128, 8 * 512])
    m_mla = din("m_mla", [128, 8 * 512])
    c_ident = din("c_ident", [128, 128])
    c_tri = din("c_tri", [128, 128])
    c_iota = din("c_iota", [128, 16])
    y_out = Buf(nc.dram_tensor("y", [OWN, D], F32, kind="ExternalOutput").ap(), trk=False)

    HNT_ALL = scr("HNT_ALL", [D, NTA], BF16)
    HNT_OWN = scr("HNT_OWN", [D, OWN], BF16)
    CKV_T = scr("CKV_T", [KVL, NTA], F32)
    KR1 = scr("KR1", [ROPE, NTA], F32)
    KRR_T = scr("KRR_T", [ROPE, NTA], BF16)
    KSB_T = scr("KSB_T", [1024, NTA], BF16)
    VSB = scr("VSB", [NTA, 1024], BF16)
    CQ_T = scr("CQ_T", [QL, OWN], F32)
    QSB_T = scr("QSB_T", [1024, OWN], BF16)
    GA_T = scr("GA_T", [D, OWN], F32)
    GB_T = scr("GB_T", [D, OWN], F32)
    CQN_T = scr("CQN_T", [QL, OWN], BF16)
    CKVN_T = scr("CKVN_T", [KVL, NTA], BF16)
    QN_T = scr("QN_T", [1024, OWN], BF16)
    QR1 = scr("QR1", [512, OWN], F32)
    QR_T = scr("QR_T", [512, OWN], BF16)
    KN_T = scr("KN_T", [1024, NTA], BF16)
    VA = scr("VA", [NTA, 1024], BF16)
    TCK = scr("TCK", [64, NTA], F32)
    TSK = scr("TSK", [64, NTA], F32)
    TCQ = scr("TCQ", [128, OWN], F32)
    TSQ = scr("TSQ", [128, OWN], F32)
    YA_T = scr("YA_T", [1024, OWN], BF16)
    YB_T = scr("YB_T", [1024, OWN], BF16)
    M1_T = scr("M1_T", [D, OWN], F32)
    MG_T = scr("MG_T", [D, OWN], BF16)
    H_OWN = scr("H_OWN", [OWN, D], F32)
    XNT = scr("XNT", [D, OWN], BF16)
    PQ_T = scr("PQ_T", [D, OWN], BF16)
    UV16 = scr("UV16", [NEXP, 2 * D], BF16)

    all_tiles = [(i * 512, 512) for i in range(T // 512)] + [(T, N_META)]
    own_tiles = [(i * 512, 512) for i in range(OWN // 512)]

    with es:
        ident = k.sb([128, 128], BF16, "ident")
        tri_f = k.sb([128, 128], F32, "tri_f")
        ones_f = k.sb([128, 128], F32, "ones_f")
        ones_b = k.sb([128, 128], BF16, "ones_b")
        iota16 = k.sb([128, 16], F32, "iota16")
        cst_s = k.sb([128, 8], F32, "cst_s")
        gq_s = k.sb([128, 4], F32, "gq_s")
        gkv_s = k.sb([128, 4], F32, "gkv_s")
        eps_s = k.sb([128, 1], F32, "eps_s")
        k.dma("pool", lambda e: e.dma_start(out=ident.t[:], in_=c_ident.t[:, :]), wr=[ident])
        k.dma("sp", lambda e: e.dma_start(out=tri_f.t[:], in_=c_tri.t[:, :]), wr=[tri_f])
        k.dma("sp", lambda e: e.dma_start(out=iota16.t[:], in_=c_iota.t[:, :]), wr=[iota16])
        k.dma("sp", lambda e: e.dma_start(out=cst_s.t[:], in_=cst.t[:, :]), wr=[cst_s])
        k.dma("sp", lambda e: e.dma_start(out=gq_s.t[:], in_=gqc.t[:, :]), wr=[gq_s])
        k.dma("sp", lambda e: e.dma_start(out=gkv_s.t[:], in_=gkvc.t[:, :]), wr=[gkv_s])
        k.op("dve", lambda e: e.memset(ones_f.t[:], 1.0), wr=[ones_f])
        k.op("dve", lambda e: e.memset(ones_b.t[:], 1.0), wr=[ones_b])
        k.op("dve", lambda e: e.memset(eps_s.t[:], EPS), wr=[eps_s])

        banks = [k.ps([128, 512], F32, "bank%d" % i) for i in range(8)]

        def rstd_from_ss(ss, np_, n, sq, rstd):
            k.op("act", lambda e: e.activation(out=sq.t[:np_], in_=ss.t[:np_], func=AF.Sqrt,
                                               scale=1.0 / n, bias=eps_s.t[:np_, 0:1]),
                 rd=[ss, eps_s], wr=[sq])
            k.op("dve", lambda e: e.reciprocal(out=rstd.t[:np_], in_=sq.t[:np_]), rd=[sq], wr=[rstd])

        def phase_norm_T(src, gain_dram, groups, outT):
            with contextlib.ExitStack() as ls:
                k.es, old = ls, k.es
                gb = k.sb([128, D], F32)
                xr = Ring([k.sb([128, D], F32) for _ in range(2)])
                hr = Ring([k.sb([128, D], BF16) for _ in range(2)])
                junk = k.sb([128, D], BF16)
                gr = Ring([k.sb([128, 16, 512], BF16) for _ in range(2)])
                st = Ring([(k.sb([128, 1], F32), k.sb([128, 1], F32), k.sb([128, 1], F32)) for _ in range(2)])
                pr = Ring([(banks[0], banks[1]), (banks[2], banks[3])])
                k.dma("sp", lambda e: e.dma_start(out=gb.t[:], in_=gain_dram.t[:, :]), wr=[gb])
                outv = outT.t.rearrange("(c p) t -> p c t", p=128)
                for (t0, gt) in groups:
                    grp = gr.next()
                    ntile = (gt + 127) // 128
                    for q in range(ntile):
                        r0 = t0 + q * 128
                        np_ = min(128, gt - q * 128)
                        xt = xr.next()
                        hn = hr.next()
                        ss, sq, rstd = st.next()
                        pa, pb = pr.next()
                        k.dma("sp", lambda e: e.dma_start(out=xt.t[:np_], in_=src.t[r0:r0 + np_, :]), rd=[src], wr=[xt])
                        k.op("act", lambda e: e.activation(out=junk.t[:np_], in_=xt.t[:np_], func=AF.Square,
                                                           accum_out=ss.t[:np_, 0:1]), rd=[xt], wr=[junk, ss])
                        rstd_from_ss(ss, np_, D, sq, rstd)
                        k.op("dve", lambda e: e.scalar_tensor_tensor(out=hn.t[:np_], in0=xt.t[:np_],
                                                                     scalar=rstd.t[:np_, 0:1], in1=gb.t[:np_],
                                                                     op0=ALU.mult, op1=ALU.mult),
                             rd=[xt, rstd, gb], wr=[hn])
                        for half, pbank in ((0, pa), (1, pb)):
                            pv = pbank.t[:].bitcast(BF16)
                            for c8 in range(8):
                                c = half * 8 + c8
                                k.op("pe", lambda e: e.transpose(out=pv[:, c8 * 128:c8 * 128 + np_],
                                                                 in_=hn.t[:np_, c * 128:(c + 1) * 128],
                                                                 identity=ident.t[:np_, :np_]),
                                     rd=[hn, ident], wr=[pbank], sig=(c8 == 7))
                            src_v = pv.rearrange("p (c t) -> p c t", t=128)[:, :, :np_]
                            dst_v = grp.t[:, half * 8:(half + 1) * 8, q * 128:q * 128 + np_]
                            if half == 0:
                                k.op("act", lambda e: e.copy(out=dst_v, in_=src_v), rd=[pbank], wr=[grp])
                            else:
                                k.op("dve", lambda e: e.tensor_copy(out=dst_v, in_=src_v), rd=[pbank], wr=[grp])
                    k.dma("act", lambda e: e.dma_start(out=outv[:, :, t0:t0 + gt], in_=grp.t[:, :, :gt]),
                          rd=[grp], wr=[outT])
                k.barrier()
                k.es = old

        def gemm(Wd, Kdim, n_lo, n_hi, actT, tiles, out, mode, epi="copy", odt=BF16,
                 ea=None, eb=None, ea_mod=None, out_row0=0):
            KC = Kdim // 128
            if True:
                wr_, ar_, orr32, orr16, tmp, ear, ebr, pr = GW
                orr = orr16 if odt == BF16 else orr32
                Wv = Wd.t.rearrange("(c p) n -> p c n", p=128)
                Av = actT.t.rearrange("(c p) t -> p c t", p=128)
                flip = [0]
                for s0 in range(n_lo, n_hi, 512):
                    ns = min(512, n_hi - s0)
                    wt = wr_.next()
                    k.dma("pool", lambda e: e.dma_start(out=wt.t[:, :KC, :ns], in_=Wv[:, :, s0:s0 + ns]), wr=[wt])
                    for (t0, tt) in tiles:
                        at = ar_.next()
                        k.dma("sp", lambda e: e.dma_start(out=at.t[:, :KC, :tt], in_=Av[:, :, t0:t0 + tt]), rd=[actT], wr=[at])
                        if mode == "fm":
                            subs = [(nb, min(128, ns - nb)) for nb in range(0, ns, 128)]
                        else:
                            subs = [(tb, min(128, tt - tb)) for tb in range(0, tt, 128)]
                        for (o0, on) in subs:
                            ps = pr.next()
                            if mode == "fm":
                                rows, cols = on, tt
                                for c in range(KC):
                                    k.op("pe", lambda e: e.matmul(ps.t[:rows, :cols], lhsT=wt.t[:, c, o0:o0 + on],
                                                                  rhs=at.t[:, c, :tt], start=(c == 0), stop=(c == KC - 1)),
                                         rd=[wt, at], wr=[ps], sig=(c == KC - 1))
                                orow = out_row0 + (s0 - n_lo) + o0
                                dst = out.t[orow:orow + rows, t0:t0 + cols]
                                erow, ecol = orow, t0
                            else:
                                rows, cols = on, ns
                                for c in range(KC):
                                    k.op("pe", lambda e: e.matmul(ps.t[:rows, :cols], lhsT=at.t[:, c, o0:o0 + on],
                                                                  rhs=wt.t[:, c, :ns], start=(c == 0), stop=(c == KC - 1)),
                                         rd=[wt, at], wr=[ps], sig=(c == KC - 1))
                                ocol = out_row0 + (s0 - n_lo)
                                dst = out.t[t0 + o0:t0 + o0 + rows, ocol:ocol + cols]
                                erow, ecol = t0 + o0, ocol
                            ob = orr.next()
                            if epi in ("mul", "muladd", "add"):
                                a_t = ear.next()
                                ar0 = erow % ea_mod if ea_mod else erow
                                k.dma("sp", lambda e: e.dma_start(out=a_t.t[:rows, :cols],
                                                                  in_=ea.t[ar0:ar0 + rows, ecol:ecol + cols]),
                                      rd=[ea], wr=[a_t])
                            if epi == "muladd":
                                b_t = ebr.next()
                                k.dma("sp", lambda e: e.dma_start(out=b_t.t[:rows, :cols],
                                                                  in_=eb.t[erow:erow + rows, ecol:ecol + cols]),
                                      rd=[eb], wr=[b_t])
                            if epi == "copy":
                                flip[0] ^= 1
                                if flip[0]:
                                    k.op("act", lambda e: e.copy(out=ob.t[:rows, :cols], in_=ps.t[:rows, :cols]),
                                         rd=[ps], wr=[ob])
                                else:
                                    k.op("dve", lambda e: e.tensor_copy(out=ob.t[:rows, :cols], in_=ps.t[:rows, :cols]),
                                         rd=[ps], wr=[ob])
                            elif epi == "sigmoid":
                                k.op("act", lambda e: e.activation(out=ob.t[:rows, :cols], in_=ps.t[:rows, :cols],
                                                                   func=AF.Sigmoid), rd=[ps], wr=[ob])
                            elif epi == "mul":
                                k.op("dve", lambda e: e.tensor_tensor(out=ob.t[:rows, :cols], in0=ps.t[:rows, :cols],
                                                                      in1=a_t.t[:rows, :cols], op=ALU.mult),
                                     rd=[ps, a_t], wr=[ob])
                            elif epi == "add":
                                k.op("dve", lambda e: e.tensor_tensor(out=ob.t[:rows, :cols], in0=ps.t[:rows, :cols],
                                                                      in1=a_t.t[:rows, :cols], op=ALU.add),
                                     rd=[ps, a_t], wr=[ob])
                            elif epi == "muladd":
                                tb_ = tmp.next()
                                k.op("dve", lambda e: e.tensor_tensor(out=tb_.t[:rows, :cols], in0=ps.t[:rows, :cols],
                                                                      in1=a_t.t[:rows, :cols], op=ALU.mult),
                                     rd=[ps, a_t], wr=[tb_])
                                k.op("pool", lambda e: e.tensor_tensor(out=ob.t[:rows, :cols], in0=tb_.t[:rows, :cols],
                                                                       in1=b_t.t[:rows, :cols], op=ALU.add),
                                     rd=[tb_, b_t], wr=[ob])
                            k.dma("act", lambda e: e.dma_start(out=dst, in_=ob.t[:rows, :cols]), rd=[ob], wr=[out])

        def phase_fm_norm(srcT, gcol, tiles, outT):
            if True:
                xr, sqr, rr, orr = FW
                pr = GW[7]
                sv = srcT.t.rearrange("(c p) t -> p c t", p=128)
                ov = outT.t.rearrange("(c p) t -> p c t", p=128)
                for (t0, tt) in tiles:
                    xt = xr.next(); sq = sqr.next(); (sr, rs) = rr.next(); ob = orr.next(); ps = pr.next()
                    k.dma("sp", lambda e: e.dma_start(out=xt.t[:, :, :tt], in_=sv[:, :, t0:t0 + tt]), rd=[srcT], wr=[xt])
                    k.op("act", lambda e: e.activation(out=sq.t[:, :, :tt], in_=xt.t[:, :, :tt], func=AF.Square),
                         rd=[xt], wr=[sq])
                    for c in range(4):
                        k.op("pe", lambda e: e.matmul(ps.t[:, :tt], lhsT=ones_f.t[:, :], rhs=sq.t[:, c, :tt],
                                                      start=(c == 0), stop=(c == 3)),
                             rd=[ones_f, sq], wr=[ps], sig=(c == 3))
                    k.op("act", lambda e: e.activation(out=sr.t[:, :tt], in_=ps.t[:, :tt], func=AF.Sqrt,
                                                       scale=1.0 / 512, bias=eps_s.t[:, 0:1]), rd=[ps, eps_s], wr=[sr])
                    k.op("dve", lambda e: e.reciprocal(out=rs.t[:, :tt], in_=sr.t[:, :tt]), rd=[sr], wr=[rs])
                    for c in range(4):
                        k.op("dve", lambda e: e.scalar_tensor_tensor(out=ob.t[:, c, :tt], in0=xt.t[:, c, :tt],
                                                                     scalar=gcol.t[:, c:c + 1], in1=rs.t[:, :tt],
                                                                     op0=ALU.mult, op1=ALU.mult),
                             rd=[xt, gcol, rs], wr=[ob])
                    k.dma("act", lambda e: e.dma_start(out=ov[:, :, t0:t0 + tt], in_=ob.t[:, :, :tt]), rd=[ob], wr=[outT])

        def phase_tables(pos, nP, NT, tc_out, ts_out):
            C1 = 6.28125
            C2 = 2.0 * math.pi - C1
            with contextlib.ExitStack() as ls:
                k.es, old = ls, k.es
                W_ = 512
                pi_ = Ring([k.sb([128, W_], I32) for _ in range(2)])
                ni_ = Ring([k.sb([128, W_], I32) for _ in range(2)])
                f = [Ring([k.sb([128, W_], F32) for _ in range(2)]) for _ in range(8)]
                for t0 in range(0, NT, W_):
                    tt = min(W_, NT - t0)
                    pi = pi_.next(); ni = ni_.next()
                    pf, ang, nf, x1, x2, xs, xc, o = [r.next() for r in f]
                    sl = lambda b: b.t[:nP, :tt]
                    k.dma("sp", lambda e: e.dma_start(out=sl(pi), in_=pos.t[:nP, t0:t0 + tt]), wr=[pi])
                    k.op("dve", lambda e: e.tensor_copy(out=sl(pf), in_=sl(pi)), rd=[pi], wr=[pf])
                    k.op("dve", lambda e: e.tensor_scalar(out=sl(ang), in0=sl(pf), scalar1=float(N_META),
                                                          scalar2=cst_s.t[:nP, 0:1], op0=ALU.add, op1=ALU.mult),
                         rd=[pf, cst_s], wr=[ang])
                    k.op("dve", lambda e: e.tensor_scalar(out=sl(ni), in0=sl(ang), scalar1=1.0 / (2.0 * math.pi),
                                                          scalar2=None, op0=ALU.mult), rd=[ang], wr=[ni])
                    k.op("dve", lambda e: e.tensor_copy(out=sl(nf), in_=sl(ni)), rd=[ni], wr=[nf])
                    k.op("dve", lambda e: e.scalar_tensor_tensor(out=sl(x1), in0=sl(nf), scalar=-C1, in1=sl(ang),
                                                                 op0=ALU.mult, op1=ALU.add), rd=[nf, ang], wr=[x1])
                    k.op("dve", lambda e: e.scalar_tensor_tensor(out=sl(x2), in0=sl(nf), scalar=-C2, in1=sl(x1),
                                                                 op0=ALU.mult, op1=ALU.add), rd=[nf, x1], wr=[x2])
                    k.op("dve", lambda e: e.tensor_scalar(out=sl(o), in0=sl(x2), scalar1=math.pi, scalar2=-2.0 * math.pi,
                                                          op0=ALU.is_gt, op1=ALU.mult), rd=[x2], wr=[o])
                    k.op("dve", lambda e: e.tensor_tensor(out=sl(xs), in0=sl(x2), in1=sl(o), op=ALU.add), rd=[x2, o], wr=[xs])
                    k.op("dve", lambda e: e.tensor_scalar(out=sl(pf), in0=sl(x2), scalar1=math.pi / 2, scalar2=None,
                                                          op0=ALU.add), rd=[x2], wr=[pf])
                    k.op("dve", lambda e: e.tensor_scalar(out=sl(o), in0=sl(pf), scalar1=math.pi, scalar2=-2.0 * math.pi,
                                                          op0=ALU.is_gt, op1=ALU.mult), rd=[pf], wr=[o])
                    k.op("dve", lambda e: e.tensor_tensor(out=sl(xc), in0=sl(pf), in1=sl(o), op=ALU.add), rd=[pf, o], wr=[xc])
                    k.op("dve", lambda e: e.tensor_scalar(out=sl(xs), in0=sl(xs), scalar1=math.pi, scalar2=-math.pi,
                                                          op0=ALU.min, op1=ALU.max), rd=[xs], wr=[xs])
                    k.op("dve", lambda e: e.tensor_scalar(out=sl(xc), in0=sl(xc), scalar1=math.pi, scalar2=-math.pi,
                                                          op0=ALU.min, op1=ALU.max), rd=[xc], wr=[xc])
                    k.op("act", lambda e: e.activation(out=sl(o), in_=sl(xc), func=AF.Sin), rd=[xc], wr=[o])
                    k.dma("sp", lambda e: e.dma_start(out=tc_out.t[:nP, t0:t0 + tt], in_=sl(o)), rd=[o], wr=[tc_out])
                    k.op("act", lambda e: e.activation(out=sl(x1), in_=sl(xs), func=AF.Sin), rd=[xs], wr=[x1])
                    k.op("dve", lambda e: e.tensor_scalar(out=sl(x2), in0=sl(x1), scalar1=cst_s.t[:nP, 1:2],
                                                          scalar2=None, op0=ALU.mult), rd=[x1, cst_s], wr=[x2])
                    k.dma("sp", lambda e: e.dma_start(out=ts_out.t[:nP, t0:t0 + tt], in_=sl(x2)), rd=[x2], wr=[ts_out])
                k.barrier()
                k.es = old

        def alloc_gemm_ws():
            return (Ring([k.sb([128, 16, 512], BF16) for _ in range(2)]),
                    Ring([k.sb([128, 16, 512], BF16) for _ in range(2)]),
                    Ring([k.sb([128, 512], F32) for _ in range(3)]),
                    Ring([k.sb([128, 512], BF16) for _ in range(3)]),
                    Ring([k.sb([128, 512], F32) for _ in range(2)]),
                    Ring([k.sb([128, 512], F32) for _ in range(3)]),
                    Ring([k.sb([128, 512], F32) for _ in range(3)]),
                    Ring(banks))

        def alloc_fm_ws():
            return (Ring([k.sb([128, 4, 512], F32) for _ in range(2)]),
                    Ring([k.sb([128, 4, 512], F32) for _ in range(2)]),
                    Ring([(k.sb([128, 512], F32), k.sb([128, 512], F32)) for _ in range(2)]),
                    Ring([k.sb([128, 4, 512], BF16) for _ in range(2)]))

        phase_norm_T(x_all, g1b, all_tiles, HNT_ALL)
        phase_norm_T(x_own, g1b, own_tiles, HNT_OWN)
        phase_tables(pos_all, 64, NTA, TCK, TSK)
        phase_tables(pos_own, 128, OWN, TCQ, TSQ)

        sec = contextlib.ExitStack()
        k.es, old_es = sec, k.es
        GW = alloc_gemm_ws()
        FW = alloc_fm_ws()
        gemm(W["wckv"], D, 0, KVL, HNT_ALL, all_tiles, CKV_T, "fm", odt=F32)
        gemm(W["wkr"], D, 0, ROPE, HNT_ALL, all_tiles, KR1, "fm", epi="mul", odt=F32, ea=TCK)
        gemm(W["wkrs"], D, 0, ROPE, HNT_ALL, all_tiles, KRR_T, "fm", epi="muladd", odt=BF16, ea=TSK, eb=KR1)
        gemm(W["wksb"], D, 0, 1024, HNT_ALL, all_tiles, KSB_T, "fm", odt=BF16)
        gemm(W["wvsb"], D, 0, 1024, HNT_ALL, all_tiles, VSB, "tm", odt=BF16)
        gemm(W["wcq"], D, 0, QL, HNT_OWN, own_tiles, CQ_T, "fm", odt=F32)
        gemm(W["wqsb"], D, 0, 1024, HNT_OWN, own_tiles, QSB_T, "fm", odt=BF16)
        gemm(W["wga"], D, 0, D, HNT_OWN, own_tiles, GA_T, "fm", epi="sigmoid", odt=F32)
        gemm(W["wgb"], D, 0, D, HNT_OWN, own_tiles, GB_T, "fm", epi="sigmoid", odt=F32)

        phase_fm_norm(CQ_T, gq_s, own_tiles, CQN_T)
        phase_fm_norm(CKV_T, gkv_s, all_tiles, CKVN_T)
        gemm(W["wuqn"], QL, 0, 1024, CQN_T, own_tiles, QN_T, "fm", odt=BF16)
        gemm(W["wuqr"], QL, 0, 512, CQN_T, own_tiles, QR1, "fm", epi="mul", odt=F32, ea=TCQ, ea_mod=128)
        gemm(W["wuqrs"], QL, 0, 512, CQN_T, own_tiles, QR_T, "fm", epi="muladd", odt=BF16, ea=TSQ, ea_mod=128, eb=QR1)
        gemm(W["wuk"], KVL, 0, 1024, CKVN_T, all_tiles, KN_T, "fm", odt=BF16)
        gemm(W["wuv"], KVL, 0, 1024, CKVN_T, all_tiles, VA, "tm", odt=BF16)
        k.barrier()
        sec.close()
        k.es = old_es

        sc_mla = 1.0 / math.sqrt(NOPE + ROPE)
        sc_sb = 1.0 / math.sqrt(128.0)
        with contextlib.ExitStack() as ls:
            k.es, old = ls, k.es
            NBA = NB + 1
            msb_f = k.sb([128, 8, 512], BF16)
            mml_f = k.sb([128, 8, 512], BF16)
            k.dma("pool", lambda e: e.dma_start(out=msb_f.t[:], in_=m_sb.t.rearrange("p (j q) -> p j q", q=512)), wr=[msb_f])
            k.dma("pool", lambda e: e.dma_start(out=mml_f.t[:], in_=m_mla.t.rearrange("p (j q) -> p j q", q=512)), wr=[mml_f])
            KR = k.sb([64, NTA], BF16)
            k.dma("sp", lambda e: e.dma_start(out=KR.t[:], in_=KRR_T.t[:, :]), rd=[KRR_T], wr=[KR])
            hbs = [dict(QS=k.sb([128, OWN], BF16), KS=k.sb([128, NTA], BF16), VS=k.sb([128, NBA, 128], BF16),
                        QN=k.sb([128, OWN], BF16), QR=k.sb([64, OWN], BF16), KN=k.sb([128, NTA], BF16),
                        VAh=k.sb([128, NBA, 128], BF16), YA=k.sb([128, OWN], BF16), YB=k.sb([128, OWN], BF16))
                   for _ in range(2)]
            Pr = Ring([k.sb([128, 512], BF16) for _ in range(3)])
            Er = Ring([k.sb([128, 512], F32) for _ in range(3)])
            Lr = Ring([k.sb([128, 512], F32) for _ in range(3)])
            Xr = Ring([k.sb([128, 512], F32) for _ in range(2)])
            Ar = Ring([k.sb([128, 512], BF16) for _ in range(3)])
            Lacc = Ring([k.sb([128, 512], F32) for _ in range(3)])
            rz = k.sb([128, 512], F32)
            Sm = Ring([banks[0], banks[1]])
            Ss = Ring([banks[2], banks[3]])
            pI = banks[4]
            Ob, Zb, Yb = banks[5], banks[6], banks[7]
            VSBv = VSB.t[0:T, :].rearrange("(j p) n -> p j n", p=128)
            VAv = VA.t[0:T, :].rearrange("(j p) n -> p j n", p=128)
            stg_r = Ring([k.sb([128, 2 * D], BF16) for _ in range(3)])

            def conv_chunk(c):
                e0 = c * 128
                stg = stg_r.next()
                k.dma("pool", lambda e: e.dma_start(out=stg.t[:, 0:D], in_=peer_u.t[e0:e0 + 128, :]), wr=[stg])
                k.dma("pool", lambda e: e.dma_start(out=stg.t[:, D:2 * D], in_=peer_v.t[e0:e0 + 128, :]), wr=[stg])
                k.dma("sp", lambda e: e.dma_start(out=UV16.t[e0:e0 + 128, :], in_=stg.t[:]), rd=[stg], wr=[UV16])

            def load_head(h):
                hb = hbs[h % 2]
                hs = slice(h * 128, (h + 1) * 128)
                k.dma("sp", lambda e: e.dma_start(out=hb["QS"].t[:], in_=QSB_T.t[hs, :]), rd=[QSB_T], wr=[hb["QS"]])
                k.dma("sp", lambda e: e.dma_start(out=hb["KS"].t[:], in_=KSB_T.t[hs, :]), rd=[KSB_T], wr=[hb["KS"]])
                k.dma("sp", lambda e: e.dma_start(out=hb["VS"].t[:, 0:NB, :], in_=VSBv[:, :, hs]), rd=[VSB], wr=[hb["VS"]])
                k.dma("sp", lambda e: e.dma_start(out=hb["VS"].t[:N_META, NB, :], in_=VSB.t[T:NTA, hs]), rd=[VSB], wr=[hb["VS"]])
                k.dma("sp", lambda e: e.dma_start(out=hb["QN"].t[:], in_=QN_T.t[hs, :]), rd=[QN_T], wr=[hb["QN"]])
                k.dma("sp", lambda e: e.dma_start(out=hb["QR"].t[:], in_=QR_T.t[h * 64:(h + 1) * 64, :]), rd=[QR_T], wr=[hb["QR"]])
                k.dma("sp", lambda e: e.dma_start(out=hb["KN"].t[:], in_=KN_T.t[hs, :]), rd=[KN_T], wr=[hb["KN"]])
                k.dma("sp", lambda e: e.dma_start(out=hb["VAh"].t[:, 0:NB, :], in_=VAv[:, :, hs]), rd=[VA], wr=[hb["VAh"]])
                k.dma("sp", lambda e: e.dma_start(out=hb["VAh"].t[:N_META, NB, :], in_=VA.t[T:NTA, hs]), rd=[VA], wr=[hb["VAh"]])

            steps = []
            for h in range(NH):
                for I in range(NQT):
                    sl_ = [(8 * I + jj, jj) for jj in range(7, -1, -1)] + [(j, None) for j in range(8 * I - 1, -1, -1)] + [(NB, None)]
                    for si, (j, jj) in enumerate(sl_):
                        steps.append(dict(h=h, I=I, j=j, jj=jj, first=(si == 0), last=(si == len(sl_) - 1),
                                          kp=(N_META if j == NB else 128), k0=j * 128,
                                          qs=slice(I * 512, (I + 1) * 512), hb=hbs[h % 2]))

            def stA(st):
                hb, kp, k0, qs = st["hb"], st["kp"], st["k0"], st["qs"]
                pS = Sm.next(); pS2 = Ss.next()
                st["pS"], st["pS2"] = pS, pS2
                k.op("pe", lambda e: e.matmul(pS.t[:kp, :], lhsT=hb["KN"].t[:, k0:k0 + kp], rhs=hb["QN"].t[:, qs],
                                              start=True, stop=False), rd=[hb["KN"], hb["QN"]], wr=[pS], sig=False)
                k.op("pe", lambda e: e.matmul(pS.t[:kp, :], lhsT=KR.t[:, k0:k0 + kp], rhs=hb["QR"].t[:, qs],
                                              start=False, stop=True), rd=[KR, hb["QR"]], wr=[pS])
                k.op("pe", lambda e: e.matmul(pS2.t[:kp, :], lhsT=hb["KS"].t[:, k0:k0 + kp], rhs=hb["QS"].t[:, qs],
                                              start=True, stop=True), rd=[hb["KS"], hb["QS"]], wr=[pS2])

            def stB_act(st):
                kp, jj = st["kp"], st["jj"]
                pS, pS2 = st["pS"], st["pS2"]
                Pb = Pr.next(); Ef = Er.next(); Lf = Lr.next()
                st["Pb"], st["Ef"], st["Lf"] = Pb, Ef, Lf
                k.op("act", lambda e: e.activation(out=Pb.t[:kp], in_=pS.t[:kp, :], func=AF.Exp, scale=sc_mla),
                     rd=[pS], wr=[Pb])
                k.op("act", lambda e: e.activation(out=Ef.t[:kp], in_=pS2.t[:kp, :], func=AF.Exp, scale=sc_sb),
                     rd=[pS2], wr=[Ef])
                k.op("act", lambda e: e.activation(out=Lf.t[:kp], in_=Ef.t[:kp], func=AF.Ln, bias=ones_f.t[:kp, 0:1]),
                     rd=[Ef, ones_f], wr=[Lf])
                if jj is not None:
                    k.op("pool", lambda e: e.tensor_tensor(out=Pb.t[:kp], in0=Pb.t[:kp], in1=mml_f.t[:kp, jj, :],
                                                           op=ALU.mult), rd=[Pb, mml_f], wr=[Pb])
                    k.op("pool", lambda e: e.tensor_tensor(out=Lf.t[:kp], in0=Lf.t[:kp], in1=msb_f.t[:kp, jj, :],
                                                           op=ALU.mult), rd=[Lf, msb_f], wr=[Lf])

            def stB_pe(st, la_prev):
                hb, kp, j = st["hb"], st["kp"], st["j"]
                first, last = st["first"], st["last"]
                Pb, Lf = st["Pb"], st["Lf"]
                k.op("pe", lambda e: e.matmul(Ob.t[:, :], lhsT=hb["VAh"].t[:kp, j, :], rhs=Pb.t[:kp],
                                              start=first, stop=last), rd=[hb["VAh"], Pb], wr=[Ob], sig=False)
                k.op("pe", lambda e: e.matmul(Zb.t[:, :], lhsT=ones_b.t[:kp, :], rhs=Pb.t[:kp],
                                              start=first, stop=last), rd=[ones_b, Pb], wr=[Zb])
                k.op("pe", lambda e: e.matmul(pI.t[:kp, :], lhsT=tri_f.t[:kp, :kp], rhs=Lf.t[:kp],
                                              start=True, stop=first), rd=[tri_f, Lf], wr=[pI], sig=first)
                if not first:
                    k.op("pe", lambda e: e.matmul(pI.t[:kp, :], lhsT=ones_f.t[:, :kp], rhs=la_prev.t[:, :],
                                                  start=False, stop=True), rd=[ones_f, la_prev], wr=[pI])
                la = None
                if not last:
                    la = Lacc.next()
                    if first:
                        k.op("dve", lambda e: e.tensor_copy(out=la.t[:], in_=Lf.t[:]), rd=[Lf], wr=[la])
                    else:
                        k.op("dve", lambda e: e.tensor_tensor(out=la.t[:], in0=la_prev.t[:], in1=Lf.t[:], op=ALU.add),
                             rd=[la_prev, Lf], wr=[la])
                return la

            def stC(st):
                kp, jj = st["kp"], st["jj"]
                Ef = st["Ef"]
                Xf = Xr.next(); Ab = Ar.next()
                st["Ab"] = Ab
                k.op("act", lambda e: e.activation(out=Xf.t[:kp], in_=pI.t[:kp, :], func=AF.Exp, scale=-1.0),
                     rd=[pI], wr=[Xf])
                k.op("dve", lambda e: e.tensor_tensor(out=Ab.t[:kp], in0=Ef.t[:kp], in1=Xf.t[:kp], op=ALU.mult),
                     rd=[Ef, Xf], wr=[Ab])
                if jj is not None:
                    k.op("pool", lambda e: e.tensor_tensor(out=Ab.t[:kp], in0=Ab.t[:kp], in1=msb_f.t[:kp, jj, :],
                                                           op=ALU.mult), rd=[Ab, msb_f], wr=[Ab])

            def fin_mla(st):
                hb, qs = st["hb"], st["qs"]
                k.op("dve", lambda e: e.reciprocal(out=rz.t[:], in_=Zb.t[:, :]), rd=[Zb], wr=[rz])
                k.op("dve", lambda e: e.tensor_tensor(out=hb["YA"].t[:, qs], in0=Ob.t[:, :], in1=rz.t[:], op=ALU.mult),
                     rd=[Ob, rz], wr=[hb["YA"]])

            def stD(st):
                hb, kp, j, qs = st["hb"], st["kp"], st["j"], st["qs"]
                Ab = st["Ab"]
                k.op("pe", lambda e: e.matmul(Yb.t[:, :], lhsT=hb["VS"].t[:kp, j, :], rhs=Ab.t[:kp],
                                              start=st["first"], stop=st["last"]), rd=[hb["VS"], Ab], wr=[Yb])
                if st["last"]:
                    k.op("act", lambda e: e.copy(out=hb["YB"].t[:, qs], in_=Yb.t[:, :]), rd=[Yb], wr=[hb["YB"]])
                    if st["I"] == NQT - 1:
                        h = st["h"]
                        hs = slice(h * 128, (h + 1) * 128)
                        k.dma("act", lambda e: e.dma_start(out=YA_T.t[hs, :], in_=hb["YA"].t[:]), rd=[hb["YA"]], wr=[YA_T])
                        k.dma("act", lambda e: e.dma_start(out=YB_T.t[hs, :], in_=hb["YB"].t[:]), rd=[hb["YB"]], wr=[YB_T])

            load_head(0)
            n = len(steps)
            stA(steps[0])
            stB_act(steps[0])
            la_prev = None
            NCH = NEXP // 128
            for i in range(n):
                st = steps[i]
                for c in range((NCH * i) // n, (NCH * (i + 1)) // n):
                    conv_chunk(c)
                if i + 1 < n:
                    stA(steps[i + 1])
                    stB_act(steps[i + 1])
                la_prev = stB_pe(st, la_prev)
                if st["last"]:
                    fin_mla(st)
                stC(st)
                if i >= 1:
                    stD(steps[i - 1])
                    sp_ = steps[i - 1]
                    if sp_["first"] and sp_["I"] == 0 and sp_["h"] + 1 < NH:
                        load_head(sp_["h"] + 1)
            stD(steps[n - 1])
            k.barrier()
            k.es = old

        sec = contextlib.ExitStack()
        k.es, old_es = sec, k.es
        GW = alloc_gemm_ws()
        gemm(W["wa"], 1024, 0, D, YA_T, own_tiles, M1_T, "fm", epi="mul", odt=F32, ea=GA_T)
        gemm(W["wb"], 1024, 0, D, YB_T, own_tiles, MG_T, "fm", epi="muladd", odt=BF16, ea=GB_T, eb=M1_T)
        gemm(W["wout"], D, 0, D, MG_T, own_tiles, H_OWN, "tm", epi="add", odt=F32, ea=x_own)
        k.barrier()
        sec.close()
        k.es = old_es
        phase_norm_T(H_OWN, g2b, own_tiles, XNT)
        sec = contextlib.ExitStack()
        k.es, old_es = sec, k.es
        GW = alloc_gemm_ws()
        gemm(W["pwq"], D, 0, D, XNT, own_tiles, PQ_T, "fm", odt=BF16)
        k.barrier()
        sec.close()
        k.es = old_es

        with contextlib.ExitStack() as ls:
            k.es, old = ls, k.es
            kT = k.sb([128, 16, 128], BF16)
            k.dma("pool", lambda e: e.dma_start(out=kT.t[:], in_=keysT.t.rearrange("p (g n) -> p g n", n=128)), wr=[kT])
            g2s = k.sb([128, D], F32)
            gfs = k.sb([128, D], F32)
            k.dma("sp", lambda e: e.dma_start(out=g2s.t[:], in_=g2b.t[:, :]), wr=[g2s])
            k.dma("sp", lambda e: e.dma_start(out=gfs.t[:], in_=gfb.t[:, :]), wr=[gfs])
            qTr = Ring([k.sb([128, 16, 128], BF16) for _ in range(2)])
            hr_ = Ring([k.sb([128, D], F32) for _ in range(2)])
            xnr = Ring([k.sb([128, D], F32) for _ in range(2)])
            junkr = Ring([k.sb([128, D], BF16) for _ in range(3)])
            junka = k.sb([128, D], BF16)

            def subs(b, n):
                return [Buf(b.t) for _ in range(n)]
            s_sb = k.sb([128, 16, 128], F32)
            s2_sb = k.sb([128, 16, 128], F32)
            vtop = k.sb([128, 16, 16], F32)
            itop = k.sb([128, 16, 16], U32)
            itopf = k.sb([128, 16, 16], F32)
            cand = k.sb([128, 8, 256], F32)
            cand2 = k.sb([128, 8, 256], F32)
            vfin = k.sb([128, 8, 16], F32)
            pfin = k.sb([128, 8, 16], U32)
            pi_u = k.sb([128, 8, 16], U32)
            pj_u = k.sb([128, 8, 16], U32)
            pi_f = k.sb([128, 8, 16], F32)
            pj_f = k.sb([128, 8, 16], F32)
            I1 = k.sb([128, 128], F32)
            I2 = k.sb([128, 128], F32)
            idxf = k.sb([128, 128], F32)
            idxr = Ring([k.sb([128, 128], I32) for _ in range(2)])
            ge = k.sb([128, 8, 16], F32)
            gz = k.sb([128, 8], F32)
            grz = k.sb([128, 8], F32)
            gater = Ring([k.sb([128, 128], F32) for _ in range(2)])
            GS = 2
            NG = 128 // GS
            hd_t = k.sb([128, 128], F32)
            gl_t = k.sb([128, 128], F32)
            av_t = k.sb([128, 128], F32)
            hd_s = subs(hd_t, 128)
            s_sb4 = subs(s_sb, 4); s2g = subs(s2_sb, 16); vtg = subs(vtop, 16); itg = subs(itop, 16)
            cdh = subs(cand, 8); cd2h = subs(cand2, 8); vfh = subs(vfin, 8); pfh = subs(pfin, 8)
            gl_g = [Buf(gl_t.t) for _ in range(NG)]
            av_g = [Buf(av_t.t) for _ in range(NG)]
            dgr = Ring([k.sb([128, 128], BF16) for _ in range(6)])
            gsr = Ring([k.sb([128, 128], BF16) for _ in range(10)])
            gbuf = Ring([k.sb([128, 2 * D], BF16) for _ in range(8)])
            st = Ring([(k.sb([128, 1], F32), k.sb([128, 1], F32), k.sb([128, 1], F32)) for _ in range(2)])
            yt = k.sb([128, D], F32)
            accb = banks[4:8]
            PQv = PQ_T.t.rearrange("(g p) t -> p g t", p=128)
            NTB = OWN // 128

            def peer_prep_gen(tb, holder):
                r0 = tb * 128
                qT = qTr.next(); ht = hr_.next(); xn = xnr.next()
                k.dma("sp", lambda e: e.dma_start(out=qT.t[:], in_=PQv[:, :, r0:r0 + 128]), rd=[PQ_T], wr=[qT])
                yield
                k.dma("sp", lambda e: e.dma_start(out=ht.t[:], in_=H_OWN.t[r0:r0 + 128, :]), rd=[H_OWN], wr=[ht])
                yield
                ss, sq, rstd = st.next()
                k.op("act", lambda e: e.activation(out=junka.t[:], in_=ht.t[:], func=AF.Square, accum_out=ss.t[:, 0:1]),
                     rd=[ht], wr=[junka, ss])
                yield
                rstd_from_ss(ss, 128, D, sq, rstd)
                k.op("dve", lambda e: e.scalar_tensor_tensor(out=xn.t[:], in0=ht.t[:], scalar=rstd.t[:, 0:1], in1=g2s.t[:],
                                                             op0=ALU.mult, op1=ALU.mult), rd=[ht, rstd, g2s], wr=[xn])
                yield
                for g4 in range(4):
                    pb = banks[g4]
                    for gg in range(4):
                        g = g4 * 4 + gg
                        k.op("pe", lambda e: e.matmul(pb.t[:, gg * 128:(gg + 1) * 128], lhsT=qT.t[:, g, :], rhs=kT.t[:, g, :],
                                                      start=True, stop=True), rd=[qT, kT], wr=[pb], sig=(gg == 3))
                        yield
                    k.op("act", lambda e: e.copy(out=s_sb.t[:, g4 * 4:(g4 + 1) * 4, :],
                                                 in_=pb.t[:, :].rearrange("p (g n) -> p g n", n=128)), rd=[pb], wr=[s_sb4[g4]])
                    yield
                for g in range(16):
                    k.op("dve", lambda e: e.max(out=vtop.t[:, g, 0:8], in_=s_sb.t[:, g, :]), rd=[s_sb4[g // 4]], wr=[vtg[g]])
                    yield
                for g in range(16):
                    k.op("dve", lambda e: e.max_index(out=itop.t[:, g, 0:8], in_max=vtop.t[:, g, 0:8], in_values=s_sb.t[:, g, :]),
                         rd=[s_sb4[g // 4], vtg[g]], wr=[itg[g]])
                    yield
                for g in range(16):
                    k.op("dve", lambda e: e.match_replace(out=s2_sb.t[:, g, :], in_to_replace=vtop.t[:, g, 0:8],
                                                          in_values=s_sb.t[:, g, :], imm_value=NEG), rd=[s_sb4[g // 4], vtg[g]], wr=[s2g[g]])
                    yield
                for g in range(16):
                    k.op("dve", lambda e: e.max(out=vtop.t[:, g, 8:16], in_=s2_sb.t[:, g, :]), rd=[s2g[g]], wr=[vtg[g]])
                    yield
                for g in range(16):
                    k.op("dve", lambda e: e.max_index(out=itop.t[:, g, 8:16], in_max=vtop.t[:, g, 8:16], in_values=s2_sb.t[:, g, :]),
                         rd=[s2g[g], vtg[g]], wr=[itg[g]])
                    yield
                k.op("dve", lambda e: e.tensor_copy(out=itopf.t[:], in_=itop.t[:]), rd=itg, wr=[itopf])
                yield
                vt4 = vtop.t[:].rearrange("p (h c) k -> p h c k", c=2)
                it4 = itopf.t[:].rearrange("p (h c) k -> p h c k", c=2)
                for h in range(NH):
                    k.op("dve", lambda e: e.tensor_tensor(out=cand.t[:, h, :].rearrange("p (i j) -> p i j", j=16),
                                                          in0=vt4[:, h, 0, :].unsqueeze(2).to_broadcast([128, 16, 16]),
                                                          in1=vt4[:, h, 1, :].unsqueeze(1).to_broadcast([128, 16, 16]),
                                                          op=ALU.add), rd=[vtg[2 * h], vtg[2 * h + 1]], wr=[cdh[h]])
                    yield
                for h in range(NH):
                    k.op("dve", lambda e: e.max(out=vfin.t[:, h, 0:8], in_=cand.t[:, h, :]), rd=[cdh[h]], wr=[vfh[h]])
                    yield
                for h in range(NH):
                    k.op("dve", lambda e: e.max_index(out=pfin.t[:, h, 0:8], in_max=vfin.t[:, h, 0:8], in_values=cand.t[:, h, :]),
                         rd=[cdh[h], vfh[h]], wr=[pfh[h]])
                    yield
                for h in range(NH):
                    k.op("dve", lambda e: e.match_replace(out=cand2.t[:, h, :], in_to_replace=vfin.t[:, h, 0:8],
                                                          in_values=cand.t[:, h, :], imm_value=NEG), rd=[cdh[h], vfh[h]], wr=[cd2h[h]])
                    yield
                for h in range(NH):
                    k.op("dve", lambda e: e.max(out=vfin.t[:, h, 8:16], in_=cand2.t[:, h, :]), rd=[cd2h[h]], wr=[vfh[h]])
                    yield
                for h in range(NH):
                    k.op("dve", lambda e: e.max_index(out=pfin.t[:, h, 8:16], in_max=vfin.t[:, h, 8:16], in_values=cand2.t[:, h, :]),
                         rd=[cd2h[h], vfh[h]], wr=[pfh[h]])
                    yield
                k.op("dve", lambda e: e.tensor_scalar(out=pi_u.t[:], in0=pfin.t[:], scalar1=4, scalar2=None,
                                                      op0=ALU.logical_shift_right), rd=pfh, wr=[pi_u])
                yield
                k.op("dve", lambda e: e.tensor_scalar(out=pj_u.t[:], in0=pfin.t[:], scalar1=15, scalar2=None,
                                                      op0=ALU.bitwise_and), rd=pfh, wr=[pj_u])
                yield
                k.op("dve", lambda e: e.tensor_copy(out=pi_f.t[:], in_=pi_u.t[:]), rd=[pi_u], wr=[pi_f])
                yield
                k.op("dve", lambda e: e.tensor_copy(out=pj_f.t[:], in_=pj_u.t[:]), rd=[pj_u], wr=[pj_f])
                yield
                for (pf_, c_, Iout, ohb, ohs) in ((pi_f, 0, I1, cand, cdh), (pj_f, 1, I2, cand2, cd2h)):
                    ohv = ohb.t[:].rearrange("p h (k i) -> p h k i", i=16)
                    for h in range(NH):
                        k.op("dve", lambda e: e.tensor_tensor(out=ohv[:, h, :, :],
                                                              in0=pf_.t[:, h, :].unsqueeze(2).to_broadcast([128, 16, 16]),
                                                              in1=iota16.t[:, :].unsqueeze(1).to_broadcast([128, 16, 16]),
                                                              op=ALU.is_equal), rd=[pf_, iota16], wr=[ohs[h]])
                        yield
                    for h in range(NH):
                        k.op("dve", lambda e: e.tensor_tensor(out=ohv[:, h, :, :], in0=ohv[:, h, :, :],
                                                              in1=it4[:, h, c_, :].unsqueeze(1).to_broadcast([128, 16, 16]),
                                                              op=ALU.mult), rd=[ohs[h], itopf], wr=[ohs[h]])
                        yield
                    k.op("dve", lambda e: e.tensor_reduce(out=Iout.t[:, :], in_=ohb.t[:].rearrange("p h (k i) -> p (h k) i", i=16),
                                                          axis=AX.X, op=ALU.add), rd=ohs, wr=[Iout])
                    yield
                k.op("dve", lambda e: e.scalar_tensor_tensor(out=idxf.t[:], in0=I1.t[:], scalar=128.0, in1=I2.t[:],
                                                             op0=ALU.mult, op1=ALU.add), rd=[I1, I2], wr=[idxf])
                yield
                idx = idxr.next()
                k.op("dve", lambda e: e.tensor_copy(out=idx.t[:], in_=idxf.t[:]), rd=[idxf], wr=[idx])
                yield
                k.op("dve", lambda e: e.tensor_tensor(out=ge.t[:], in0=vfin.t[:], in1=vfin.t[:, :, 0:1].to_broadcast([128, 8, 16]),
                                                      op=ALU.subtract), rd=vfh, wr=[ge])
                yield
                k.op("act", lambda e: e.activation(out=ge.t[:], in_=ge.t[:], func=AF.Exp), rd=[ge], wr=[ge])
                yield
                k.op("dve", lambda e: e.tensor_reduce(out=gz.t[:, :], in_=ge.t[:], axis=AX.X, op=ALU.add), rd=[ge], wr=[gz])
                yield
                k.op("dve", lambda e: e.reciprocal(out=grz.t[:], in_=gz.t[:]), rd=[gz], wr=[grz])
                yield
                gate = gater.next()
                k.op("dve", lambda e: e.tensor_tensor(out=gate.t[:].rearrange("p (h k) -> p h k", k=16), in0=ge.t[:],
                                                      in1=grz.t[:].unsqueeze(2).to_broadcast([128, 8, 16]), op=ALU.mult),
                     rd=[ge, grz], wr=[gate])
                yield
                holder.update(dict(ht=ht, xn=xn, idx=idx, gate=gate, r0=r0))

            def grp_fetch(pp, g):
                bufs = []
                for s_ in range(g * GS, (g + 1) * GS):
                    gb_ = gbuf.next()
                    k.dma("pool", lambda e: e.indirect_dma_start(out=gb_.t[:], out_offset=None, in_=UV16.t[:, :],
                                                                 in_offset=bass.IndirectOffsetOnAxis(ap=pp["idx"].t[:, s_:s_ + 1], axis=0)),
                          rd=[pp["idx"], UV16], wr=[gb_])
                    jk = junkr.next()
                    k.op("dve", lambda e: e.scalar_tensor_tensor(out=jk.t[:], in0=gb_.t[:, 0:D], scalar=1.0, in1=pp["xn"].t[:],
                                                                 op0=ALU.mult, op1=ALU.mult, accum_out=hd_t.t[:, s_:s_ + 1]),
                         rd=[gb_, pp["xn"]], wr=[jk, hd_s[s_]])
                    gs_ = gsr.next()
                    k.op("dve", lambda e: e.tensor_scalar(out=gs_.t[:], in0=ident.t[:], scalar1=pp["gate"].t[:, s_:s_ + 1],
                                                          scalar2=None, op0=ALU.mult), rd=[ident, pp["gate"]], wr=[gs_])
                    bufs.append((gb_, gs_))
                return bufs

            def grp_finish(pp, g, bufs):
                gs = slice(g * GS, (g + 1) * GS)
                k.op("act", lambda e: e.activation(out=gl_g[g].t[:, gs], in_=hd_t.t[:, gs], func=AF.Gelu),
                     rd=hd_s[g * GS:(g + 1) * GS], wr=[gl_g[g]])
                for q, s_ in enumerate(range(g * GS, (g + 1) * GS)):
                    gb_, gs_ = bufs[q]
                    dg = dgr.next()
                    k.op("act", lambda e: e.activation(out=dg.t[:], in_=gs_.t[:], func=AF.Copy, scale=gl_g[g].t[:, s_:s_ + 1]),
                         rd=[gs_, gl_g[g]], wr=[dg])
                    for n4 in range(4):
                        k.op("pe", lambda e: e.matmul(accb[n4].t[:, :], lhsT=dg.t[:, :], rhs=gb_.t[:, D + n4 * 512:D + (n4 + 1) * 512],
                                                      start=(s_ == 0), stop=(s_ == 127)), rd=[dg, gb_], wr=[accb[n4]], sig=(n4 == 3))

            def peer_out(pp):
                for n4 in range(4):
                    cs = slice(n4 * 512, (n4 + 1) * 512)
                    k.op("dve", lambda e: e.tensor_tensor(out=yt.t[:, cs], in0=accb[n4].t[:, :], in1=pp["ht"].t[:, cs], op=ALU.add),
                         rd=[accb[n4], pp["ht"]], wr=[yt])
                ss, sq, rstd = st.next()
                k.op("act", lambda e: e.activation(out=junka.t[:], in_=yt.t[:], func=AF.Square, accum_out=ss.t[:, 0:1]),
                     rd=[yt], wr=[junka, ss])
                rstd_from_ss(ss, 128, D, sq, rstd)
                k.op("dve", lambda e: e.scalar_tensor_tensor(out=yt.t[:], in0=yt.t[:], scalar=rstd.t[:, 0:1], in1=gfs.t[:],
                                                             op0=ALU.mult, op1=ALU.mult), rd=[yt, rstd, gfs], wr=[yt])
                r0 = pp["r0"]
                k.dma("sp", lambda e: e.dma_start(out=y_out.t[r0:r0 + 128, :], in_=yt.t[:]), rd=[yt], wr=[y_out])

            pp = {}
            for _ in peer_prep_gen(0, pp):
                pass
            for tb in range(NTB):
                pp_next = {}
                gen = peer_prep_gen(tb + 1, pp_next) if tb + 1 < NTB else None
                prev = None
                for g in range(NG + 1):
                    cur = grp_fetch(pp, g) if g < NG else None
                    if prev is not None:
                        grp_finish(pp, g - 1, prev)
                    prev = cur
                    if gen is not None and g >= 2:
                        for _ in range(5):
                            if next(gen, "done") == "done":
                                gen = None
                                break
                if gen is not None:
                    for _ in gen:
                        pass
                peer_out(pp)
                pp = pp_next
            k.barrier()
            k.es = old
    return nc


def _consts(T):
    p = np.arange(128)
    inv_freq = (10000.0 ** (-(np.arange(32, dtype=np.float32)) / 32.0)).astype(np.float32)
    cst = np.zeros((128, 8), np.float32)
    cst[:, 0] = inv_freq[p % 32]
    cst[:, 1] = np.where((p % 64) < 32, -1.0, 1.0)
    ident = np.eye(128, dtype=np.float32)
    tri = (p[:, None] >= p[None, :]).astype(np.float32)
    iota = np.tile(np.arange(16, dtype=np.float32)[None], (128, 1))
    return cst, ident, tri, iota


def _masks(s):
    kl = np.arange(128)[:, None]
    ql = np.arange(128)[None, :]
    msb = np.zeros((128, 8, 512), np.float32)
    mml = np.zeros((128, 8, 512), np.float32)
    for jj in range(8):
        for r in range(4):
            g = 2 * r + s
            cs = slice(r * 128, (r + 1) * 128)
            if jj < g:
                msb[:, jj, cs] = 1.0
                mml[:, jj, cs] = 1.0
            elif jj == g:
                msb[:, jj, cs] = (kl < ql)
                mml[:, jj, cs] = ((kl // CHUNK) <= (ql // CHUNK))
    return msb.reshape(128, 8 * 512), mml.reshape(128, 8 * 512)


def _rep(v, n=128):
    return np.ascontiguousarray(np.broadcast_to(np.asarray(v, np.float32)[None, :], (n, v.shape[0])))


def _col(v):
    v = np.asarray(v, np.float32)
    return np.ascontiguousarray(v.reshape(-1, 128).T)


def prep_inputs(inp, B, T):
    f = lambda a: np.ascontiguousarray(np.asarray(a, np.float32))
    x = np.asarray(inp["x"], np.float32)
    pos = np.asarray(inp["positions"]).astype(np.int32)
    meta = f(inp["meta_tokens"])
    w_in = f(inp["w_in"][0])
    o = np.cumsum([0, 512, 512, 64, 1024, 1024, 1024, 2048, 2048])
    seg = [w_in[:, o[i]:o[i + 1]] for i in range(8)]
    sw64 = np.concatenate([np.arange(32, 64), np.arange(0, 32)])
    uq = f(inp["mla_w_uq"][0]).reshape(QL, NH, NOPE + ROPE)
    uqn = uq[:, :, :NOPE].reshape(QL, NH * NOPE)
    uqr = uq[:, :, NOPE:]
    cst, ident, tri, iota = _consts(T)
    shared = {
        "cst": cst, "c_ident": ident, "c_tri": tri, "c_iota": iota,
        "g1b": _rep(inp["norm_mix_g"][0]), "g2b": _rep(inp["norm_ffn_g"][0]), "gfb": _rep(inp["final_norm_g"]),
        "gqc": _col(inp["mla_q_norm_g"][0]), "gkvc": _col(inp["mla_kv_norm_g"][0]),
        "wcq": f(seg[0]), "wckv": f(seg[1]), "wkr": f(seg[2]), "wkrs": f(seg[2][:, sw64]),
        "wqsb": f(seg[3]), "wksb": f(seg[4]), "wvsb": f(seg[5]), "wga": f(seg[6]), "wgb": f(seg[7]),
        "wuqn": f(uqn), "wuqr": f(uqr.reshape(QL, NH * ROPE)), "wuqrs": f(uqr[:, :, sw64].reshape(QL, NH * ROPE)),
        "wuk": f(inp["mla_w_uk"][0]), "wuv": f(inp["mla_w_uv"][0]),
        "wa": f(inp["w_branch_a"][0]), "wb": f(inp["w_branch_b"][0]), "wout": f(inp["w_out"][0]),
        "pwq": f(inp["peer_w_q"][0]),
        "keysT": f(np.transpose(np.asarray(inp["peer_sub_keys"][0], np.float32).reshape(16, 128, 128), (2, 0, 1)).reshape(128, 16 * 128)),
        "peer_u": f(inp["peer_u"][0]), "peer_v": f(inp["peer_v"][0]),
    }
    maps = []
    metapos = (np.arange(N_META) - N_META).astype(np.int32)
    for c in range(2 * B):
        b, s = c // 2, c % 2
        xb = x[b].reshape(T // 128, 128, D)
        pb = pos[b].reshape(T // 128, 128)
        m = dict(shared)
        m["x_all"] = np.ascontiguousarray(np.concatenate([x[b], meta], axis=0))
        m["x_own"] = np.ascontiguousarray(xb[s::2].reshape(T // 2, D))
        pa = np.concatenate([pos[b], metapos])
        m["pos_all"] = np.ascontiguousarray(np.broadcast_to(pa[None], (64, T + N_META))).astype(np.int32)
        po = pb[s::2].reshape(T // 2)
        m["pos_own"] = np.ascontiguousarray(np.broadcast_to(po[None], (128, T // 2))).astype(np.int32)
        msb, mml = _masks(s)
        m["m_sb"] = msb
        m["m_mla"] = mml
        maps.append(m)
    return maps


def run(inp, debug=()):
    x = np.asarray(inp["x"])
    B, T, _ = x.shape
    nc = build(T, debug=debug)
    maps = prep_inputs(inp, B, T)
    res = run_bass_kernel_spmd(nc, maps, core_ids=list(range(2 * B)))
    out = np.zeros((B, T, D), np.float32)
    for c in range(2 * B):
        b, s = c // 2, c % 2
        out[b].reshape(T // 128, 128, D)[s::2] = res.results[c]["y"].reshape(T // 256, 128, D)
    return out, res


def kernel(**inputs):
    out, _ = run(inputs)
    return out
```
